# Optimizing a Trainium2 kernel written in Bass

```python
import jax, jax.numpy as jnp
from jax import lax
import numpy as np

D_MODEL = 1024
BATCH = 2
SEQ = 8192
DEPTH = 2

HEAD_DIM = 64
MOBA_HEADS = 8
MOBA_BLOCK = 256
MOBA_TOPK = 3
MOBA_QCHUNK = 128
MOBA_W = MOBA_HEADS * HEAD_DIM
DIL_GROUPS = ((128, 1), (512, 4), (2048, 16))
DIL_HEADS_PER_GROUP = 4
DIL_HEADS = DIL_HEADS_PER_GROUP * len(DIL_GROUPS)
DIL_W = DIL_HEADS * HEAD_DIM
DIL_OUT_W = DIL_HEADS_PER_GROUP * HEAD_DIM
CONV_WIDTH = 512
CONV_K = 3
N_BRANCH = 3
SPLIT_SIZES = (MOBA_W, MOBA_W, MOBA_W, DIL_W, DIL_W, DIL_W,
               CONV_WIDTH, CONV_WIDTH, CONV_WIDTH, N_BRANCH * D_MODEL)
IN_COLS = sum(SPLIT_SIZES)
D_FF = 2816
FFN_CONV_K = 3
DN_ALPHA = (2 * DEPTH) ** 0.25
DN_BETA = (8 * DEPTH) ** -0.25
LN_EPS = 1e-5

kernel_name = "hybrid_moba_dilated_shortconv_deepnorm"


def layer_norm(x, g, b):
    xf = x.astype(jnp.float32)
    mu = jnp.mean(xf, axis=-1, keepdims=True)
    var = jnp.mean(jnp.square(xf - mu), axis=-1, keepdims=True)
    return ((xf - mu) * lax.rsqrt(var + LN_EPS)).astype(x.dtype) * g + b


def causal_dwconv(u, w):
    k = w.shape[0]
    s = u.shape[1]
    up = jnp.pad(u, ((0, 0), (k - 1, 0), (0, 0)))
    y = up[:, k - 1:k - 1 + s] * w[0]
    for j in range(1, k):
        y = y + up[:, k - 1 - j:k - 1 - j + s] * w[j]
    return y


def moba_attention(q, k, v):
    b, s, h, dh = q.shape
    nb = -(-s // MOBA_BLOCK)
    sp = nb * MOBA_BLOCK
    pad = ((0, 0), (0, sp - s), (0, 0), (0, 0))
    q, k, v = (jnp.pad(t, pad).transpose(0, 2, 1, 3) for t in (q, k, v))
    kb = k.reshape(b, h, nb, MOBA_BLOCK, dh)
    vb = v.reshape(b, h, nb, MOBA_BLOCK, dh)
    scale = HEAD_DIM ** -0.5
    kmean = jnp.mean(kb, axis=3)
    gate = jnp.einsum('bhsd,bhnd->bhsn', q, kmean).astype(jnp.float32)
    q_blk = jnp.arange(sp) // MOBA_BLOCK
    past = jnp.arange(nb)[None, :] < q_blk[:, None]
    gate = jnp.where(past, gate, -jnp.inf)
    n_sel = max(1, min(MOBA_TOPK, nb - 1))
    _, sel = lax.top_k(gate, n_sel)
    sel_valid = sel < q_blk[:, None]

    nc = sp // MOBA_QCHUNK
    qc_len = MOBA_QCHUNK

    def to_chunks(t):
        t = t.reshape(b, h, nc, qc_len, *t.shape[3:])
        return jnp.moveaxis(t, 2, 0)

    bi = jnp.arange(b)[:, None, None, None]
    hi = jnp.arange(h)[None, :, None, None]
    n_g = n_sel * MOBA_BLOCK

    def chunk_attn(args):
        qc, selc, validc, c = args
        q_pos = c * qc_len + jnp.arange(qc_len)
        blk = (c * qc_len) // MOBA_BLOCK
        kg = kb[bi, hi, selc]
        vg = vb[bi, hi, selc]
        ko = lax.dynamic_index_in_dim(kb, blk, axis=2, keepdims=False)
        vo = lax.dynamic_index_in_dim(vb, blk, axis=2, keepdims=False)
        s_sel = jnp.einsum('bhqd,bhqknd->bhqkn', qc, kg).astype(jnp.float32) * scale
        s_sel = jnp.where(validc[..., None], s_sel, -jnp.inf).reshape(b, h, qc_len, n_g)
        s_own = jnp.einsum('bhqd,bhnd->bhqn', qc, ko).astype(jnp.float32) * scale
        k_pos = blk * MOBA_BLOCK + jnp.arange(MOBA_BLOCK)
        s_own = jnp.where(k_pos[None, :] <= q_pos[:, None], s_own, -jnp.inf)
        p = jax.nn.softmax(jnp.concatenate([s_sel, s_own], axis=-1), axis=-1).astype(v.dtype)
        p_sel = p[..., :n_g].reshape(b, h, qc_len, n_sel, MOBA_BLOCK)
        return (jnp.einsum('bhqkn,bhqknd->bhqd', p_sel, vg)
                + jnp.einsum('bhqn,bhnd->bhqd', p[..., n_g:], vo))

    out = lax.map(chunk_attn, (to_chunks(q), to_chunks(sel), to_chunks(sel_valid),
                               jnp.arange(nc)))
    out = jnp.moveaxis(out, 0, 2).reshape(b, h, sp, dh)[:, :, :s]
    return out.transpose(0, 2, 1, 3).reshape(b, s, h * dh)


def dilated_group_attention(q, k, v, window, dilation):
    b, s, h, dh = q.shape
    span = window // dilation
    blk = span
    l = s // dilation
    nb = -(-l // blk)
    lp = nb * blk
    scale = HEAD_DIM ** -0.5

    def split(t):
        t = t.reshape(b, l, dilation, h, dh).transpose(0, 2, 3, 1, 4)
        return jnp.pad(t, ((0, 0), (0, 0), (0, 0), (0, lp - l), (0, 0)))

    def band(t):
        tb = t.reshape(b, dilation, h, nb, blk, dh)
        prev = jnp.pad(tb, ((0, 0), (0, 0), (0, 0), (1, 0), (0, 0), (0, 0)))[:, :, :, :-1]
        return jnp.concatenate([prev, tb], axis=4)

    qs, ks, vs = split(q), split(k), split(v)
    qb = qs.reshape(b, dilation, h, nb, blk, dh)
    kb, vb = band(ks), band(vs)
    sc = jnp.einsum('brhnqd,brhnkd->brhnqk', qb, kb).astype(jnp.float32) * scale
    q_pos = jnp.arange(nb)[:, None, None] * blk + jnp.arange(blk)[None, :, None]
    k_pos = jnp.arange(nb)[:, None, None] * blk - blk + jnp.arange(2 * blk)[None, None, :]
    dist = q_pos - k_pos
    mask = (dist >= 0) & (dist <= span) & (k_pos >= 0)
    sc = jnp.where(mask, sc, -jnp.inf)
    lse = jax.nn.logsumexp(sc, axis=-1)
    p = jnp.exp(sc - lse[..., None]).astype(v.dtype)
    o = jnp.einsum('brhnqk,brhnkd->brhnqd', p, vb)

    def merge(t):
        t = t.reshape(b, dilation, h, lp, *t.shape[5:])[:, :, :, :l]
        t = jnp.moveaxis(t, 3, 1)
        return t.reshape(b, s, h, *t.shape[4:])

    return merge(o), merge(lse)


def dilated_mixture(q, k, v):
    b, s = q.shape[:2]
    outs, lses = [], []
    for g, (window, dilation) in enumerate(DIL_GROUPS):
        hs = slice(g * DIL_HEADS_PER_GROUP, (g + 1) * DIL_HEADS_PER_GROUP)
        o, lse = dilated_group_attention(q[:, :, hs], k[:, :, hs], v[:, :, hs], window, dilation)
        outs.append(o)
        lses.append(lse)
    wts = jax.nn.softmax(jnp.stack(lses, axis=0), axis=0).astype(q.dtype)
    out = jnp.einsum('gbsh,gbshd->bshd', wts, jnp.stack(outs, axis=0))
    return out.reshape(b, s, DIL_OUT_W)


def mixer_sublayer(x, w_in, w_short_conv, w_moba_proj, w_dil_proj, w_conv_proj, w_mix_out):
    b, s, _ = x.shape
    split_points = tuple(int(i) for i in np.cumsum(SPLIT_SIZES)[:-1])
    qa, ka, va, qd, kd, vd, b_gate, c_gate, hc, gate_logits = jnp.split(
        x @ w_in, split_points, axis=-1)
    heads = lambda t, n: t.reshape(b, s, n, HEAD_DIM)
    y_moba = moba_attention(heads(qa, MOBA_HEADS), heads(ka, MOBA_HEADS), heads(va, MOBA_HEADS))
    y_conv = b_gate * causal_dwconv(c_gate * hc, w_short_conv)
    y_dil = dilated_mixture(heads(qd, DIL_HEADS), heads(kd, DIL_HEADS), heads(vd, DIL_HEADS))
    gates = jax.nn.sigmoid(gate_logits).reshape(b, s, N_BRANCH, D_MODEL)
    merged = (gates[:, :, 0] * (y_moba @ w_moba_proj)
              + gates[:, :, 1] * (y_conv @ w_conv_proj)
              + gates[:, :, 2] * (y_dil @ w_dil_proj))
    return merged @ w_mix_out


def conv_ffn_sublayer(x, w_up, w_ffn_conv, b_ffn_conv, w_down):
    u = causal_dwconv(x @ w_up, w_ffn_conv) + b_ffn_conv
    gate, val = jnp.split(u, 2, axis=-1)
    return (jax.nn.silu(gate) * val) @ w_down


def setup_inputs(seed: int = 0) -> dict:
    key = jax.random.key(seed)
    ks = jax.random.split(key, 16)
    nrm = lambda k, shape, sc: jax.random.normal(k, shape, jnp.float32) * sc
    L = DEPTH
    return {
        "x": nrm(ks[0], (BATCH, SEQ, D_MODEL), 1.0),
        "w_in": nrm(ks[1], (L, D_MODEL, IN_COLS), D_MODEL ** -0.5),
        "w_short_conv": nrm(ks[2], (L, CONV_K, CONV_WIDTH), CONV_K ** -0.5),
        "w_moba_proj": nrm(ks[3], (L, MOBA_W, D_MODEL), MOBA_W ** -0.5),
        "w_dil_proj": nrm(ks[4], (L, DIL_OUT_W, D_MODEL), DIL_OUT_W ** -0.5),
        "w_conv_proj": nrm(ks[5], (L, CONV_WIDTH, D_MODEL), CONV_WIDTH ** -0.5),
        "w_mix_out": nrm(ks[6], (L, D_MODEL, D_MODEL), DN_BETA * D_MODEL ** -0.5),
        "ln1_g": 1.0 + nrm(ks[7], (L, D_MODEL), 0.02),
        "ln1_b": nrm(ks[8], (L, D_MODEL), 0.02),
        "w_up": nrm(ks[9], (L, D_MODEL, 2 * D_FF), D_MODEL ** -0.5),
        "w_ffn_conv": nrm(ks[10], (L, FFN_CONV_K, 2 * D_FF), FFN_CONV_K ** -0.5),
        "b_ffn_conv": nrm(ks[11], (L, 2 * D_FF), 0.02),
        "w_down": nrm(ks[12], (L, D_FF, D_MODEL), DN_BETA * D_FF ** -0.5),
        "ln2_g": 1.0 + nrm(ks[13], (L, D_MODEL), 0.02),
        "ln2_b": nrm(ks[14], (L, D_MODEL), 0.02),
    }


def reference(x, w_in, w_short_conv, w_moba_proj, w_dil_proj, w_conv_proj, w_mix_out,
              ln1_g, ln1_b, w_up, w_ffn_conv, b_ffn_conv, w_down, ln2_g, ln2_b):
    for l in range(DEPTH):
        mix = mixer_sublayer(x, w_in[l], w_short_conv[l], w_moba_proj[l], w_dil_proj[l],
                             w_conv_proj[l], w_mix_out[l])
        x = layer_norm(DN_ALPHA * x + mix, ln1_g[l], ln1_b[l])
        ffn = conv_ffn_sublayer(x, w_up[l], w_ffn_conv[l], b_ffn_conv[l], w_down[l])
        x = layer_norm(DN_ALPHA * x + ffn, ln2_g[l], ln2_b[l])
    return x
```

```python
import contextlib
import numpy as np
import concourse.bass as bass
import concourse.mybir as mybir
from concourse.bass_utils import run_bass_kernel_spmd

F32 = mybir.dt.float32
BF16 = mybir.dt.bfloat16
I32 = mybir.dt.int32
AF = mybir.ActivationFunctionType
ALU = mybir.AluOpType
AX = mybir.AxisListType

D_MODEL = 1024
SEQ = 8192
DEPTH = 2
TOK = 2048
IN_COLS = 8448
D_FF = 2816
ALPHA = (2 * DEPTH) ** 0.25
LN_EPS = 1e-5
NEG = -30000.0

R_QA, R_KA, R_QD, R_KD, R_VM, R_VD, R_X1 = 0, 512, 1024, 1792, 2560, 3072, 4096
NIDX = 64
IDX_K, IDX_Q, IDX_V = 0, 8, 16
IDX_DQ, IDX_DK, IDX_DV = 20, 32, 44
IDX_YM, IDX_YD, IDX_HALO = 52, 56, 60


CH1, CH2 = 256, 64


def gx1(r, row):
    return (row // CH1) * (4 * CH1) + r * CH1 + (row % CH1)


def gx2(r, row):
    return (row // CH2) * (4 * CH2) + r * CH2 + (row % CH2)


def host_idx(c):
    j = c % 4
    p = np.arange(128)
    T = np.zeros((128, NIDX), np.int32)
    for h in range(2):
        for r in range(4):
            T[:, IDX_K + h * 4 + r] = gx1(r, R_KA + 128 * j + 64 * h + p)
            T[:, IDX_Q + h * 4 + r] = gx1(r, R_QA + 128 * j + 64 * h + p)
    for r in range(4):
        T[:, IDX_V + r] = gx1(r, R_VM + 128 * j + p)
    for g in range(3):
        hd = 4 * g + j
        for r in range(4):
            T[:, IDX_DQ + g * 4 + r] = gx1(r, R_QD + 64 * hd + p)
            T[:, IDX_DK + g * 4 + r] = gx1(r, R_KD + 64 * hd + p)
    for r in range(4):
        for hf in range(2):
            row = gx1(r, R_VD + 256 * j + 2 * p + hf)
            T[:, IDX_DV + r * 2 + hf] = row if hf == 0 else 2 * row
    for r in range(4):
        T[:, IDX_YM + r] = gx2(r, p) * 4 + j
        T[:, IDX_YD + r] = gx2(r, 128 + (p % 64)) * 4 + j
    T[:, IDX_HALO] = ((j - 1) * 256 + p) if j > 0 else (128 + p)
    return T


class Tok:
    __slots__ = ("name", "writer", "readers")

    def __init__(self, name=""):
        self.name = name
        self.writer = None
        self.readers = []


class Op:
    __slots__ = ("eng", "fn", "deps", "need_inc", "semkey", "semval", "kind", "name")


class Prog:
    ENGS = ("pe", "act", "dve", "pool", "sp")

    def __init__(self, nc, info_ap=None):
        self.nc = nc
        self.ops = []
        self.toks = {}
        self.sems = {}
        self.counts = {}
        self.info_ap = info_ap
        self.dyn = {}

    def tok(self, name):
        t = self.toks.get(name)
        if t is None:
            t = Tok(name)
            self.toks[name] = t
        return t

    def _toks(self, xs):
        out = []
        if isinstance(xs, (str, Tok)):
            xs = [xs]
        for x in xs or ():
            if isinstance(x, Tok):
                out.append(x)
            elif isinstance(x, (list, tuple)):
                out.extend(self._toks(x))
            else:
                out.append(self.tok(x))
        return out

    def op(self, eng, fn, r=(), w=(), dma=None, cc=None, name=""):
        o = Op()
        o.eng = eng
        o.fn = fn
        o.kind = "dma" if dma is not None else ("cc" if cc is not None else "eng")
        o.semkey = ("dma", dma) if dma is not None else (("cc", cc) if cc is not None else ("eng", eng))
        o.need_inc = o.kind != "eng"
        o.semval = None
        o.name = name
        deps = []
        rt = self._toks(r)
        wt = self._toks(w)
        for t in rt:
            if t.writer is not None:
                deps.append(t.writer)
        for t in wt:
            if t.writer is not None:
                deps.append(t.writer)
            deps.extend(t.readers)
        for t in rt:
            t.readers.append(o)
        for t in wt:
            t.writer = o
            t.readers = []
        seen = set()
        dd = []
        for d in deps:
            if id(d) in seen or d is o:
                continue
            seen.add(id(d))
            if d.kind == "eng" and o.kind == "eng" and d.eng == "pe" and o.eng == "pe":
                continue
            dd.append(d)
        o.deps = dd
        for d in dd:
            d.need_inc = True
        self.ops.append(o)
        return o

    def emit(self):
        nc = self.nc
        if not hasattr(self, "totals"):
            self.totals = {}
            self.handles = {}
            self.pool = []
            self.eng_sem = {}
        prev_totals = dict(self.totals)
        for e in self.ENGS:
            last = [o for o in self.ops if o.eng == e and o.kind == "eng"]
            if last:
                last[-1].need_inc = True
        keymap = {}

        def sem_for(key):
            if key in keymap:
                return keymap[key]
            if key[0] == "eng":
                nm = self.eng_sem.get(key)
                if nm is None:
                    nm = "se_%s" % key[1]
                    self.handles[nm] = nc.alloc_semaphore(name=nm)
                    self.totals[nm] = 0
                    self.eng_sem[key] = nm
            else:
                if self.pool:
                    nm = self.pool.pop()
                else:
                    nm = "sd_%d" % len(self.handles)
                    self.handles[nm] = nc.alloc_semaphore(name=nm)
                    self.totals[nm] = 0
            keymap[key] = nm
            return nm

        for o in self.ops:
            if o.need_inc:
                nm = sem_for(o.semkey)
                step = 16 if o.kind == "dma" else 1
                self.totals[nm] += step
                o.semval = self.totals[nm]
                o.semkey = nm
        for o in self.ops:
            if not o.need_inc:
                o.semkey = None
        per_eng = {e: [o for o in self.ops if o.eng == e] for e in self.ENGS}
        totals = dict(self.totals)
        handles = self.handles
        with nc.Block() as block:

            def run(engname, engobj):
                waited = {}
                if per_eng[engname] or engname == "sp":
                    for k, v in prev_totals.items():
                        if v > 0:
                            engobj.wait_ge(handles[k], v)
                            waited[k] = v
                for o in per_eng[engname]:
                    for d in o.deps:
                        k = d.semkey
                        if waited.get(k, 0) >= d.semval:
                            continue
                        engobj.wait_ge(handles[k], d.semval)
                        waited[k] = d.semval
                    ins = o.fn(engobj)
                    if o.need_inc:
                        ins.then_inc(handles[o.semkey], 16 if o.kind == "dma" else 1)
                if engname == "sp":
                    for k, v in totals.items():
                        if waited.get(k, 0) < v:
                            engobj.wait_ge(handles[k], v)

            @block.tensor
            def _(e):
                run("pe", e)

            @block.scalar
            def _(e):
                run("act", e)

            @block.vector
            def _(e):
                run("dve", e)

            @block.gpsimd
            def _(e):
                run("pool", e)

            @block.sync
            def _(e):
                run("sp", e)
        for key, nm in keymap.items():
            if key[0] != "eng":
                self.pool.append(nm)
        self.ops = []
        self.toks = {}


class Ring:
    def __init__(self, nc, es, name, shape, dtype, n, psum=False):
        self.name = name
        self.n = n
        self.i = 0
        mk = nc.psum_tensor if psum else nc.sbuf_tensor
        self.tiles = [es.enter_context(mk("%s%d" % (name, k), list(shape), dtype)) for k in range(n)]

    def next(self):
        k = self.i % self.n
        self.i += 1
        return self.tiles[k], "%s#%d" % (self.name, k)


def mm_group(P, out_ap, pairs, r, w, rk=None):
    n = len(pairs)
    for k, (l, rr) in enumerate(pairs):
        P.op("pe", lambda e, l=l, rr=rr, k=k: e.matmul(out_ap, l, rr, start=(k == 0), stop=(k == n - 1)),
             r=list(r) + (rk[k] if rk is not None else []), w=w)


class WStream:
    def __init__(self, nc, es, name, shape, nstage=2, nbf=2):
        self.stage = Ring(nc, es, name + "_st", shape, F32, nstage)
        self.bf = Ring(nc, es, name + "_bf", shape, BF16, nbf)
        self.k = 0

    def load(self, P, src_ap, sub=None, queue="sp"):
        st, stok = self.stage.next()
        bf, btok = self.bf.next()
        sl = sub if sub is not None else (lambda t: t[:])
        P.op(queue, lambda e: e.dma_start(out=sl(st), in_=src_ap), w=[stok], dma=stok)
        eng = "act" if (self.k % 2 == 0) else "dve"
        self.k += 1
        if eng == "act":
            P.op("act", lambda e: e.copy(sl(bf), sl(st)), r=[stok], w=[btok])
        else:
            P.op("dve", lambda e: e.tensor_copy(sl(bf), sl(st)), r=[stok], w=[btok])
        return bf, btok


def phase_p1(nc, P, D, layer, first, mid_hook=None):
    with contextlib.ExitStack() as es:
        xT = es.enter_context(nc.sbuf_tensor("p1_xT", [128, 8, TOK], BF16))
        win = D["w_in"][layer].rearrange("(kc p) n -> p kc n", p=128)
        es0 = es
        ws = WStream(nc, es0, "p1_w", [128, 8, 512], nstage=2, nbf=2)
        ps = Ring(nc, es0, "p1_ps", [128, 512], F32, 6, psum=True)
        es = es0.enter_context(contextlib.ExitStack())
        if first:
            xst = Ring(nc, es, "p1_xst", [128, 1024], F32, 2)
            for kc in range(8):
                for hf in range(2):
                    st, stok = xst.next()
                    P.op("sp", lambda e, st=st, kc=kc, hf=hf: e.dma_start(
                        out=st[:], in_=D["xT_f32"][kc * 128:(kc + 1) * 128, hf * 1024:(hf + 1) * 1024]),
                        w=[stok], dma=stok)
                    P.op("dve" if hf else "pool", lambda e, st=st, kc=kc, hf=hf: e.tensor_copy(
                        xT[:, kc, hf * 1024:(hf + 1) * 1024], st[:]),
                        r=[stok], w=["xT%d_%d" % (kc, hf)])
                P.op("pool", lambda e, kc=kc: e.dma_start(out=D["xT_a"][kc * 128:(kc + 1) * 128, :], in_=xT[:, kc, :]),
                     r=["xT%d_0" % kc, "xT%d_1" % kc], w=["xT_a_d%d" % kc], dma="st_xT")
            xtoks = ["xT%d_%d" % (kc, hf) for kc in range(8) for hf in range(2)]
        else:
            for kc in range(8):
                P.op("sp", lambda e, kc=kc: e.dma_start(out=xT[:, kc, :], in_=D["xT_a"][kc * 128:(kc + 1) * 128, :]),
                     w=["xT%d" % kc], dma="ld_xT%d" % kc)
            xtoks = ["xT%d" % kc for kc in range(8)]

        ost = Ring(nc, es, "p1_ost", [128, TOK], BF16, 2)
        evk = [0]

        def evac(dst_ap, src_ap, r, w):
            eng = "act" if evk[0] % 2 == 0 else "dve"
            evk[0] += 1
            if eng == "act":
                P.op("act", lambda e: e.copy(dst_ap, src_ap), r=r, w=w)
            else:
                P.op("dve", lambda e: e.tensor_copy(dst_ap, src_ap), r=r, w=w)

        fm_groups = [(0, 512, R_QA), (512, 512, R_KA), (1536, 512, R_QD), (2048, 256, R_QD + 512),
                     (2304, 512, R_KD), (2816, 256, R_KD + 512)]
        for (c0, n, row0) in fm_groups:
            wb, wtok = ws.load(P, win[:, :, c0:c0 + n], sub=(lambda t, n=n: t[:, :, 0:n]))
            for cc in range(n // 128):
                os_, otok = ost.next()
                for tt in range(4):
                    bank, btok = ps.next()
                    mm_group(P, bank[:, :],
                             [(wb[:, kc, cc * 128:(cc + 1) * 128], xT[:, kc, tt * 512:(tt + 1) * 512]) for kc in range(8)],
                             r=[wtok] + xtoks, w=[btok])
                    evac(os_[:, tt * 512:(tt + 1) * 512], bank[:, :], r=[btok], w=[otok + "q%d" % tt])
                r0 = row0 + cc * 128
                P.op("pool", lambda e, os_=os_, r0=r0: e.dma_start(out=D["x1in"][r0:r0 + 128, :], in_=os_[:]),
                     r=[otok + "q%d" % t for t in range(4)], w=["x1in_r%d" % r0], dma=otok)

        wc, wctok = ws.load(P, win[:, :, 4352:4864])
        wh, whtok = ws.load(P, win[:, :, 4864:5376])
        ust = Ring(nc, es, "p1_ust", [128, TOK], F32, 2)
        ctmp = Ring(nc, es, "p1_ctmp", [128, 512], F32, 2)
        uh = es.enter_context(nc.sbuf_tensor("p1_uh", [128, 8], F32))
        for cc in range(4):
            us, utok = ust.next()
            for tt in range(4):
                ba, batok = ps.next()
                bb, bbtok = ps.next()
                mm_group(P, ba[:, :], [(wc[:, kc, cc * 128:(cc + 1) * 128], xT[:, kc, tt * 512:(tt + 1) * 512]) for kc in range(8)],
                         r=[wctok] + xtoks, w=[batok])
                mm_group(P, bb[:, :], [(wh[:, kc, cc * 128:(cc + 1) * 128], xT[:, kc, tt * 512:(tt + 1) * 512]) for kc in range(8)],
                         r=[whtok] + xtoks, w=[bbtok])
                ct, cttok = ctmp.next()
                P.op("act", lambda e, ct=ct, ba=ba: e.copy(ct[:], ba[:, :]), r=[batok], w=[cttok])
                P.op("dve", lambda e, us=us, ct=ct, bb=bb, tt=tt: e.tensor_tensor(us[:, tt * 512:(tt + 1) * 512], ct[:], bb[:, :], ALU.mult),
                     r=[cttok, bbtok], w=[utok + "q%d" % tt])
            P.op("pool", lambda e, us=us, cc=cc: e.dma_start(out=D["u_scr"][cc * 128:(cc + 1) * 128, :], in_=us[:]),
                 r=[utok + "q%d" % t for t in range(4)], w=["u_scr%d" % cc], dma=utok)
            P.op("dve", lambda e, us=us, cc=cc: e.tensor_copy(uh[:, cc * 2:cc * 2 + 2], us[:, TOK - 2:TOK]),
                 r=[utok + "q3"], w=["uh"])
        uz = es.enter_context(nc.sbuf_tensor("p1_uz", [128, 8], F32))
        P.op("pool", lambda e: e.memset(uz[:], 0.0), w=["uz"])
        P.op("sp", lambda e: e.dma_start(out=D["uh_in"][0:128, :], in_=uh[:]), r=["uh"], w=["uh_in"], dma="st_uh")
        P.op("sp", lambda e: e.dma_start(out=D["uh_in"][128:256, :], in_=uz[:]), r=["uz"], w=["uh_inz"], dma="st_uh")
        P.emit()
        es.close()
        es = es0.enter_context(contextlib.ExitStack())
        xtoks = []
        if mid_hook is not None:
            mid_hook()
        if mid_hook is not None:
            mid_hook()

        wv = es.enter_context(nc.sbuf_tensor("p1_wv", [128, 8, 1280], BF16))
        for (c0, n, d0) in [(1024, 512, 0), (3072, 512, 512), (3584, 256, 1024)]:
            wb, wtok = ws.load(P, win[:, :, c0:c0 + n], sub=(lambda t, n=n: t[:, :, 0:n]))
            P.op("pool", lambda e, wb=wb, n=n, d0=d0: e.tensor_copy(wv[:, :, d0:d0 + n], wb[:, :, 0:n]),
                 r=[wtok], w=["wv%d" % d0])
        vm = es.enter_context(nc.sbuf_tensor("p1_vm", [128, 4, 16, 128], BF16))
        vd = es.enter_context(nc.sbuf_tensor("p1_vd", [128, 4, 16, 192], BF16))
        for t16 in range(16):
            lhs = [xT[:, kc, t16 * 128:(t16 + 1) * 128] for kc in range(8)]
            b0, t0 = ps.next()
            mm_group(P, b0[:, :], [(lhs[kc], wv[:, kc, 0:512]) for kc in range(8)], r=["wv0"] + xtoks, w=[t0])
            evac(vm[:, :, t16, :], b0[:, :].rearrange("p (j c) -> p j c", j=4), r=[t0], w=["vm%d" % t16])
        for g in range(3):
            rr = (1, 4, 16)[g]
            nbl = 16 // rr
            for u in range(16):
                rho, jbl = u // nbl, u % nbl
                st = jbl * 128 * rr + rho
                bq, tq = ps.next()
                mm_group(P, bq[:, 0:256],
                         [(xT[:, kc, st:st + 128 * rr - (rr - 1):rr], wv[:, kc, 512 + g * 256:512 + (g + 1) * 256]) for kc in range(8)],
                         r=["wv512", "wv1024"] + xtoks, w=[tq])
                evac(vd[:, :, u, g * 64:(g + 1) * 64], bq[:, 0:256].rearrange("p (j d) -> p j d", j=4), r=[tq], w=["vd%d_%d" % (g, u)])
        for j in range(4):
            P.op("sp", lambda e, j=j: e.dma_start(out=D["x1in"][R_VM + j * 128:R_VM + (j + 1) * 128, :],
                                                 in_=vm[:, j].rearrange("p a b -> p (a b)")),
                 r=["vm%d" % t for t in range(16)], w=["x1in_vm%d" % j], dma="st_v")
            P.op("sp", lambda e, j=j: e.dma_start(
                out=bass.AP(D["x1in"].tensor, (R_VD + j * 256) * TOK, [[4096, 128], [1, 3072]]),
                in_=vd[:, j].rearrange("p a b -> p (a b)")),
                r=["vd%d_%d" % (g, u) for g in range(3) for u in range(16)], w=["x1in_vd%d" % j], dma="st_v2")
        P.emit()


def dram_specs():
    S = {}
    S["w_in"] = ([DEPTH, 1024, IN_COLS], F32)
    S["xT_f32"] = ([1024, TOK], F32)
    S["x_tm"] = ([TOK, 1024], F32)
    S["xT_a"] = ([1024, TOK], BF16)
    S["xT_b"] = ([1024, TOK], BF16)
    S["x1in"] = ([R_X1, TOK], BF16)
    S["x1g"] = ([4 * R_X1, TOK], BF16)
    S["u_scr"] = ([512, TOK], F32)
    S["uh_in"] = ([256, 8], F32)
    S["uh_g"] = ([4 * 256, 8], F32)
    S["idx"] = ([128, NIDX], I32)
    S["blkind"] = ([32, SEQ], BF16)
    S["gtab"] = ([128, 3, 32, 32], F32)
    S["dmask"] = ([128, 4, 512], BF16)
    S["ident"] = ([128, 128], BF16)
    S["dilmask"] = ([128, 2, 128], BF16)
    S["x2in"] = ([192, SEQ], BF16)
    S["x2g"] = ([4 * 192, SEQ], BF16)
    S["w_moba_proj"] = ([DEPTH, 512, 1024], F32)
    S["w_dil_proj"] = ([DEPTH, 256, 1024], F32)
    S["w_conv_proj"] = ([DEPTH, 512, 1024], F32)
    S["w_mix_out"] = ([DEPTH, 1024, 1024], F32)
    S["w_up"] = ([DEPTH, 1024, 2 * D_FF], F32)
    S["w_down"] = ([DEPTH, D_FF, 1024], F32)
    for nm in ("ln1_g", "ln1_b", "ln2_g", "ln2_b"):
        S[nm] = ([DEPTH, 1024], F32)
    S["wsc"] = ([DEPTH, 128, 4, 3], F32)
    S["wfc"] = ([DEPTH, 128, 44, 3], F32)
    S["bfc"] = ([DEPTH, 128, 44], F32)
    S["xres_a"] = ([TOK, 1024], F32)
    S["xres_b"] = ([TOK, 1024], F32)
    S["out"] = ([TOK, 1024], F32)
    S["x3in"] = ([256, 16], BF16)
    S["x3g"] = ([4 * 256, 16], BF16)
    S["wup_bf"] = ([22, 128, 2048], BF16)
    S["dbg_m"] = ([1024, TOK], BF16)
    S["dbg_yc"] = ([512, TOK], BF16)
    return S


def make_D(nc, names_in, names_out, names_int=(), layers=DEPTH):
    S = dram_specs()
    D = {}
    for nm in names_in:
        shp, dt = S[nm]
        shp = list(shp)
        if shp[0] == DEPTH and len(shp) >= 2 and nm not in ("x3in",):
            shp[0] = layers
        D[nm] = nc.dram_tensor(nm, shp, dt, kind="ExternalInput").ap()
    for nm in names_out:
        shp, dt = S[nm]
        D[nm] = nc.dram_tensor(nm, list(shp), dt, kind="ExternalOutput").ap()
    for nm in names_int:
        shp, dt = S[nm]
        D[nm] = nc.dram_tensor(nm, list(shp), dt).ap()
    return D


def host_consts():
    import ml_dtypes
    C = {}
    t = np.arange(SEQ)
    C["blkind"] = (t[None, :] // 256 == np.arange(32)[:, None]).astype(np.float32).astype(ml_dtypes.bfloat16)
    n = np.arange(32)
    qb = np.arange(32)[:, None]
    past = (n[None, :] < qb).astype(np.float32)
    own = (n[None, :] == qb).astype(np.float32)
    pastbias = np.where(past > 0, 0.0, -1e30).astype(np.float32)
    past30k = (30000.0 * past).astype(np.float32)
    t2 = np.where(past > 0, -30000.0, np.where(own > 0, 0.0, -30000.0)).astype(np.float32)
    gt = np.stack([pastbias, past30k, t2], 0)
    C["gtab"] = np.ascontiguousarray(np.broadcast_to(gt[None], (128, 3, 32, 32))).astype(np.float32)
    i = np.arange(128)[:, None]
    m = np.ones((128, 4, 512), np.float32)
    for a in range(4):
        for b in range(4):
            jq = np.arange(128)[None, :]
            if a // 2 == b // 2:
                if a % 2 > b % 2:
                    blk = np.zeros((128, 128), np.float32)
                elif a % 2 == b % 2:
                    blk = (i <= jq).astype(np.float32)
                else:
                    blk = np.ones((128, 128), np.float32)
                m[:, a, b * 128:(b + 1) * 128] = blk
    C["dmask"] = m.astype(ml_dtypes.bfloat16)
    C["ident"] = np.eye(128, dtype=np.float32).astype(ml_dtypes.bfloat16)
    jq = np.arange(128)[None, :]
    dm = np.stack([(i >= jq), (i <= jq)], 1).astype(np.float32)
    C["dilmask"] = dm.astype(ml_dtypes.bfloat16)
    return C


def attn_finalize(P, acc, acctok, osb_ring, bc_ring, ones_f, ydst_ap, wtoks, nq=512):
    osb, otok = osb_ring.next()
    P.op("act", lambda e: e.copy(osb[0:65, 0:nq], acc[0:65, 0:nq]), r=[acctok], w=[otok])
    P.op("dve", lambda e: e.reciprocal(osb[64:65, 0:nq], osb[64:65, 0:nq]), r=[otok], w=[otok + "r"])
    bc, bctok = bc_ring.next()
    P.op("pe", lambda e: e.matmul(bc[0:64, 0:nq], ones_f[64:65, 0:64], osb[64:65, 0:nq], start=True, stop=True),
         r=[otok + "r"], w=[bctok])
    P.op("dve", lambda e: e.tensor_tensor(ydst_ap, osb[0:64, 0:nq], bc[0:64, 0:nq], ALU.mult),
         r=[otok, bctok], w=wtoks)


def phase_p2a(nc, P, D):
    x1g3 = D["x1g"].rearrange("(r n) t -> r n t", r=4)
    with contextlib.ExitStack() as es:
        qaug = [es.enter_context(nc.sbuf_tensor("qaug%d" % h, [96, SEQ], BF16)) for h in range(2)]
        kaug = [es.enter_context(nc.sbuf_tensor("kaug%d" % h, [96, SEQ], BF16)) for h in range(2)]
        vst = es.enter_context(nc.sbuf_tensor("vst", [128, 64, 128], BF16))
        vaug = [es.enter_context(nc.sbuf_tensor("vaug%d" % h, [128, 64, 65], BF16)) for h in range(2)]
        gtab = es.enter_context(nc.sbuf_tensor("sb_gtab", [128, 3, 32, 32], F32))
        dmask = es.enter_context(nc.sbuf_tensor("sb_dmask", [128, 4, 512], BF16))
        ident = es.enter_context(nc.sbuf_tensor("sb_ident", [128, 128], BF16))
        ones_f = es.enter_context(nc.sbuf_tensor("ones_f", [128, 64], F32))
        ystage = [es.enter_context(nc.sbuf_tensor("ystage%d" % h, [64, SEQ], BF16)) for h in range(2)]
        km = es.enter_context(nc.sbuf_tensor("km", [64, 2, 32], F32))
        kmh = es.enter_context(nc.sbuf_tensor("kmh", [64, 2, 2, 32], BF16))
        gsb = Ring(nc, es, "gsb", [128, 32], F32, 4)
        gm8 = Ring(nc, es, "gm8", [128, 8], F32, 4)
        gmm = Ring(nc, es, "gmm", [128, 32], F32, 4)
        gfb = Ring(nc, es, "gfb", [128, 32], BF16, 4)
        pT = Ring(nc, es, "pT", [128, 512], BF16, 5)
        osb = Ring(nc, es, "osb", [65, 512], F32, 2)
        ps_s = Ring(nc, es, "ps_s", [128, 512], F32, 4, psum=True)
        ps_acc = Ring(nc, es, "ps_acc", [128, 512], F32, 1, psum=True)
        ps_g = Ring(nc, es, "ps_g", [128, 512], F32, 1, psum=True)
        ps_t = Ring(nc, es, "ps_t", [128, 512], F32, 1, psum=True)
        ps_bc = Ring(nc, es, "ps_bc", [128, 512], F32, 1, psum=True)

        P.op("sp", lambda e: e.dma_start(out=gtab[:], in_=D["gtab"][:, :, :, :]), w=["gtab"], dma="c_gtab")
        P.op("sp", lambda e: e.dma_start(out=dmask[:], in_=D["dmask"][:, :, :]), w=["dmask"], dma="c_dmask")
        P.op("sp", lambda e: e.dma_start(out=ident[:], in_=D["ident"][:, :]), w=["ident"], dma="c_ident")
        P.op("pool", lambda e: e.memset(ones_f[:], 1.0), w=["ones_f"])
        sidx = es.enter_context(nc.sbuf_tensor("sb_idx", [128, NIDX], I32))
        P.op("sp", lambda e: e.dma_start(out=sidx[:], in_=D["idx"][:, :]), w=["idx"], dma="c_idx")

        def gather(dst, table, col, npart, wtok, chan, extra=()):
            P.op("pool", lambda e: e.indirect_dma_start(
                out=dst, out_offset=None, in_=table,
                in_offset=bass.IndirectOffsetOnAxis(ap=sidx[0:npart, col:col + 1], axis=0)),
                r=["idx"] + list(extra), w=[wtok], dma=chan)

        for h in range(2):
            P.op("sp", lambda e, h=h: e.dma_start(out=kaug[h][64:96, :], in_=D["blkind"][:, :]), w=["kind%d" % h], dma="c_ind%d" % h)
            for r in range(4):
                gather(kaug[h][0:64, r * TOK:(r + 1) * TOK], D["x1g"][:, :], IDX_K + h * 4 + r, 64, "k%d_%d" % (h, r), "ld_k%d_%d" % (h, r), ["x1g_c2", "x1g_c3"])
                gather(qaug[h][0:64, r * TOK:(r + 1) * TOK], D["x1g"][:, :], IDX_Q + h * 4 + r, 64, "q%d_%d" % (h, r), "ld_q%d_%d" % (h, r), ["x1g_c0", "x1g_c1"])
        for r in range(4):
            gather(vst[:, r * 16:(r + 1) * 16, :].rearrange("p a b -> p (a b)"), D["x1g"][:, :], IDX_V + r, 128, "vst%d" % r, "ld_v%d" % r, ["x1g_c10", "x1g_c11"])
        for h in range(2):
            P.op("pool", lambda e, h=h: e.memset(vaug[h][:, :, 64:65], 1.0), w=["vone%d" % h])
            for r in range(4):
                P.op("pool", lambda e, h=h, r=r: e.tensor_copy(vaug[h][:, r * 16:(r + 1) * 16, 0:64],
                                                             vst[:, r * 16:(r + 1) * 16, h * 64:(h + 1) * 64]),
                     r=["vst%d" % r], w=["v%d_%d" % (h, r)])
        for h in range(2):
            for r in range(4):
                P.op("dve", lambda e, h=h, r=r: e.tensor_reduce(
                    km[:, h, r * 8:(r + 1) * 8], kaug[h][0:64, r * TOK:(r + 1) * TOK].rearrange("p (n t) -> p n t", t=256),
                    AX.X, ALU.add), r=["k%d_%d" % (h, r)], w=["km%d_%d" % (h, r)])
            kr = ["km%d_%d" % (h, r) for r in range(4)]
            P.op("dve", lambda e, h=h: e.tensor_scalar(km[:, h, :], km[:, h, :], 1.0 / 256.0, None, op0=ALU.mult), r=kr, w=["kms%d" % h])
            P.op("dve", lambda e, h=h: e.tensor_copy(kmh[:, h, 0, :], km[:, h, :]), r=["kms%d" % h], w=["kmhi%d" % h])
            P.op("dve", lambda e, h=h: e.tensor_tensor(kmh[:, h, 1, :], km[:, h, :], kmh[:, h, 0, :], ALU.subtract),
                 r=["kms%d" % h, "kmhi%d" % h], w=["kmlo%d" % h])

        def gating(h, T):
            tb, ttok = ps_t.next()
            ftoks = []
            for c4 in range(4):
                qc = 4 * T + c4
                qb = qc // 2
                rk = qc // 16
                gp, gptok = ps_g.next()
                qsl = qaug[h][0:64, qc * 128:(qc + 1) * 128]
                P.op("pe", lambda e, gp=gp, qsl=qsl: e.matmul(gp[:, 0:32], qsl, kmh[:, h, 0, :], start=True, stop=False),
                     r=["q%d_%d" % (h, rk), "kmhi%d" % h], w=[gptok])
                P.op("pe", lambda e, gp=gp, qsl=qsl: e.matmul(gp[:, 0:32], qsl, kmh[:, h, 1, :], start=False, stop=True),
                     r=["q%d_%d" % (h, rk), "kmlo%d" % h], w=[gptok])
                g, gtok = gsb.next()
                m8, m8tok = gm8.next()
                mm, mmtok = gmm.next()
                fb, fbtok = gfb.next()
                P.op("dve", lambda e, g=g, gp=gp, qb=qb: e.tensor_tensor(g[:], gp[:, 0:32], gtab[:, 0, qb, :], ALU.add),
                     r=[gptok, "gtab"], w=[gtok])
                P.op("dve", lambda e, g=g, m8=m8: e.max(m8[:], g[:]), r=[gtok], w=[m8tok])
                P.op("dve", lambda e, g=g, m8=m8, mm=mm: e.tensor_scalar(mm[:], g[:], m8[:, 2:3], None, op0=ALU.is_ge),
                     r=[gtok, m8tok], w=[mmtok])
                P.op("dve", lambda e, mm=mm, qb=qb: e.tensor_tensor(mm[:], mm[:], gtab[:, 1, qb, :], ALU.mult),
                     r=[mmtok], w=[mmtok + "b"])
                P.op("dve", lambda e, mm=mm, fb=fb, qb=qb: e.tensor_tensor(fb[:], mm[:], gtab[:, 2, qb, :], ALU.add),
                     r=[mmtok + "b"], w=[fbtok])
                P.op("pe", lambda e, tb=tb, fb=fb, c4=c4: e.matmul(tb[0:32, c4 * 128:(c4 + 1) * 128], fb[:], ident[:], start=True, stop=True),
                     r=[fbtok, "ident"], w=[ttok + "c%d" % c4])
            P.op("act", lambda e, tb=tb: e.copy(qaug[h][64:96, T * 512:(T + 1) * 512], tb[0:32, :]),
                 r=[ttok + "c%d" % c for c in range(4)], w=["qbias%d_%d" % (h, T)])

        def attend(h, T):
            acc, acctok = ps_acc.next()
            nkc = 4 * T + 4
            LOOK = 3
            sbanks = {}

            def issue_qk(kc):
                sb_, stok = ps_s.next()
                P.op("pe", lambda e, sb_=sb_, kc=kc: e.matmul(sb_[:, :], kaug[h][0:96, kc * 128:(kc + 1) * 128],
                                                              qaug[h][0:96, T * 512:(T + 1) * 512], start=True, stop=True),
                     r=["k%d_%d" % (h, kc // 16), "kind%d" % h, "q%d_%d" % (h, T // 4), "qbias%d_%d" % (h, T)], w=[stok])
                sbanks[kc] = (sb_, stok)

            for kc in range(min(LOOK, nkc)):
                issue_qk(kc)
            for kc in range(nkc):
                sb_, stok = sbanks.pop(kc)
                pt, pttok = pT.next()
                P.op("act", lambda e, pt=pt, sb_=sb_: e.activation(out=pt[:], in_=sb_[:, :], func=AF.Exp, scale=0.125),
                     r=[stok], w=[pttok])
                if kc >= 4 * T:
                    a = kc - 4 * T
                    P.op("dve", lambda e, pt=pt, a=a: e.tensor_tensor(pt[:], pt[:], dmask[:, a, :], ALU.mult),
                         r=[pttok, "dmask"], w=[pttok])
                if kc + LOOK < nkc:
                    issue_qk(kc + LOOK)
                P.op("pe", lambda e, acc=acc, pt=pt, kc=kc: e.matmul(acc[0:65, :], vaug[h][:, kc, 0:65], pt[:],
                                                                     start=(kc == 0), stop=(kc == nkc - 1)),
                     r=[pttok, "v%d_%d" % (h, kc // 16), "vone%d" % h], w=[acctok])
            attn_finalize(P, acc, acctok, osb, ps_bc, ones_f, ystage[h][:, T * 512:(T + 1) * 512],
                          ["y%d_%d" % (h, T)])

        gating(0, 0)
        for h in range(2):
            for T in range(16):
                if T + 1 < 16:
                    gating(h, T + 1)
                elif h == 0:
                    gating(1, 0)
                attend(h, T)
            P.op("pool", lambda e, h=h: e.dma_start(out=D["x2in"][h * 64:(h + 1) * 64, :], in_=ystage[h][:]),
                 r=["y%d_%d" % (h, T) for T in range(16)], w=["x2in%d" % h], dma="st_y%d" % h)
        P.emit()


def phase_p2b(nc, P, D):
    with contextlib.ExitStack() as es:
        sidx = es.enter_context(nc.sbuf_tensor("sb_idx2", [128, NIDX], I32))
        qd = Ring(nc, es, "qd", [64, SEQ], BF16, 2)
        kd = Ring(nc, es, "kd", [64, SEQ], BF16, 2)
        vdst = es.enter_context(nc.sbuf_tensor("vdst", [128, 64, 192], BF16))
        vaug = [es.enter_context(nc.sbuf_tensor("vdaug%d" % g, [128, 64, 65], BF16)) for g in range(3)]
        accs = es.enter_context(nc.sbuf_tensor("accs", [65, SEQ], F32))
        dilmask = es.enter_context(nc.sbuf_tensor("sb_dilmask", [128, 2, 128], BF16))
        ones_f = es.enter_context(nc.sbuf_tensor("ones_f2", [128, 64], F32))
        ystage = es.enter_context(nc.sbuf_tensor("ystage_d", [64, SEQ], BF16))
        rrow = Ring(nc, es, "rrow", [65, 512], F32, 2)
        pT = Ring(nc, es, "pTd", [128, 256], BF16, 4)
        ps_s = Ring(nc, es, "psd_s", [128, 512], F32, 3, psum=True)
        ps_acc = Ring(nc, es, "psd_acc", [128, 512], F32, 3, psum=True)
        ps_bc = Ring(nc, es, "psd_bc", [128, 512], F32, 2, psum=True)

        P.op("sp", lambda e: e.dma_start(out=sidx[:], in_=D["idx"][:, :]), w=["idx"], dma="c_idx")
        P.op("sp", lambda e: e.dma_start(out=dilmask[:], in_=D["dilmask"][:, :, :]), w=["dilmask"], dma="c_dilmask")
        P.op("pool", lambda e: e.memset(ones_f[:], 1.0), w=["ones_f"])

        def gather(dst, table, col, npart, wtok, chan):
            P.op("pool", lambda e: e.indirect_dma_start(
                out=dst, out_offset=None, in_=table,
                in_offset=bass.IndirectOffsetOnAxis(ap=sidx[0:npart, col:col + 1], axis=0)),
                r=["idx"], w=[wtok], dma=chan)

        vflat = vdst[:].rearrange("p a b -> p (a b)")
        for r in range(4):
            gather(vflat[:, r * 3072:r * 3072 + 2048], D["x1g"][:, :], IDX_DV + r * 2, 128, "vdst%d_0" % r, "ld_dv%d_0" % r)
            gather(vflat[:, r * 3072 + 2048:(r + 1) * 3072], D["x1g"].rearrange("n (h c) -> (n h) c", h=2), IDX_DV + r * 2 + 1, 128, "vdst%d_1" % r, "ld_dv%d_1" % r)
        for g in range(3):
            P.op("dve", lambda e, g=g: e.memset(vaug[g][:, :, 64:65], 1.0), w=["vone%d" % g])
            for r in range(4):
                P.op("dve", lambda e, g=g, r=r: e.tensor_copy(vaug[g][:, r * 16:(r + 1) * 16, 0:64],
                                                             vdst[:, r * 16:(r + 1) * 16, g * 64:(g + 1) * 64]),
                     r=["vdst%d_0" % r, "vdst%d_1" % r], w=["v%d_%d" % (g, r)])

        def finalize_tile(T):
            sl = slice(T * 512, (T + 1) * 512)
            rw, rtok = rrow.next()
            P.op("dve", lambda e, rw=rw, sl=sl: e.reciprocal(rw[64:65, :], accs[64:65, sl]), r=["accs%d" % T], w=[rtok])
            bc, bctok = ps_bc.next()
            P.op("pe", lambda e, bc=bc, rw=rw: e.matmul(bc[0:64, :], ones_f[64:65, 0:64], rw[64:65, :], start=True, stop=True),
                 r=[rtok, "ones_f"], w=[bctok])
            P.op("dve", lambda e, bc=bc, sl=sl: e.tensor_tensor(ystage[:, sl], accs[0:64, sl], bc[0:64, :], ALU.mult),
                 r=["accs%d" % T, bctok], w=["yd%d" % T])

        import os
        GSEL = [int(c) for c in os.environ.get("DIL_GROUPS", "012")]
        for g in GSEL:
            rr = (1, 4, 16)[g]
            nbl = 16 // rr
            nbk = 64 // rr
            q, qtok = qd.next()
            k, ktok = kd.next()
            for r in range(4):
                gather(q[0:64, r * TOK:(r + 1) * TOK], D["x1g"][:, :], IDX_DQ + g * 4 + r, 64, qtok + "_%d" % r, "ld_dq%d_%d" % (g, r))
                gather(k[0:64, r * TOK:(r + 1) * TOK], D["x1g"][:, :], IDX_DK + g * 4 + r, 64, ktok + "_%d" % r, "ld_dk%d_%d" % (g, r))
            qk_r = [qtok + "_%d" % r for r in range(4)] + [ktok + "_%d" % r for r in range(4)]

            def tok_slice(rho, jb):
                st = rho + rr * 128 * jb
                return slice(st, st + 128 * rr - (rr - 1), rr)

            def vunit(rho, jb):
                R, jbl = jb // nbl, jb % nbl
                return R * 16 + rho * nbl + jbl, R

            if rr == 1:
                batches = [[(0, 4 * a + i) for i in range(4)] for a in range(16)]
            elif rr == 4:
                batches = [[(rho, jb) for rho in range(4)] for jb in range(16)]
            else:
                batches = [[(4 * a + i, jb) for i in range(4)] for jb in range(4) for a in range(4)]
            flat_units = [(bi, ui, rho, jb) for bi, batch in enumerate(batches) for ui, (rho, jb) in enumerate(batch)]
            qk_done = {}

            def issue_qk(n):
                bi, ui, rho, jb = flat_units[n]
                sb_, stok = ps_s.next()
                qs = q[0:64, tok_slice(rho, jb)]
                if jb > 0:
                    kprev = k[0:64, tok_slice(rho, jb - 1)]
                    P.op("pe", lambda e, sb_=sb_, qs=qs, kprev=kprev: e.matmul(
                        sb_[:, 0:128], kprev, qs, start=True, stop=True), r=qk_r, w=[stok])
                kown = k[0:64, tok_slice(rho, jb)]
                P.op("pe", lambda e, sb_=sb_, qs=qs, kown=kown: e.matmul(
                    sb_[:, 128:256], kown, qs, start=True, stop=True), r=qk_r, w=[stok])
                qk_done[n] = (sb_, stok)

            LOOK = 2
            for n in range(min(LOOK, len(flat_units))):
                issue_qk(n)
            n_unit = 0
            for bi, batch in enumerate(batches):
                acc, acctok = ps_acc.next()
                for ui, (rho, jb) in enumerate(batch):
                    sb_, stok = qk_done.pop(n_unit)
                    c0 = 0 if jb > 0 else 128
                    pt, pttok = pT.next()
                    P.op("act", lambda e, pt=pt, sb_=sb_, c0=c0: e.activation(out=pt[:, c0:256], in_=sb_[:, c0:256], func=AF.Exp, scale=0.125),
                         r=[stok], w=[pttok])
                    P.op("dve", lambda e, pt=pt, c0=c0: e.tensor_tensor(
                        pt[:, c0:256], pt[:, c0:256], dilmask[:].rearrange("p a b -> p (a b)")[:, c0:256], ALU.mult),
                        r=[pttok, "dilmask"], w=[pttok])
                    if n_unit + LOOK < len(flat_units):
                        issue_qk(n_unit + LOOK)
                    n_unit += 1
                    osl = acc[0:65, ui * 128:(ui + 1) * 128]
                    if jb > 0:
                        vu, R = vunit(rho, jb - 1)
                        vprev = vaug[g][:, vu, 0:65]
                        P.op("pe", lambda e, osl=osl, pt=pt, vprev=vprev: e.matmul(osl, vprev, pt[:, 0:128], start=True, stop=False),
                             r=[pttok, "v%d_%d" % (g, R), "vone%d" % g], w=[acctok])
                    vu, R = vunit(rho, jb)
                    vown = vaug[g][:, vu, 0:65]
                    P.op("pe", lambda e, osl=osl, pt=pt, vown=vown, jb=jb: e.matmul(osl, vown, pt[:, 128:256], start=(jb == 0), stop=True),
                         r=[pttok, "v%d_%d" % (g, R), "vone%d" % g], w=[acctok])
                if rr == 1:
                    dst = accs[0:65, bi * 512:(bi + 1) * 512]
                    src = acc[0:65, 0:512]
                elif rr == 4:
                    dst = accs[0:65, bi * 512:(bi + 1) * 512].rearrange("p (i rho) -> p rho i", rho=4)
                    src = acc[0:65, 0:512].rearrange("p (rho i) -> p rho i", rho=4)
                else:
                    jb, a = bi // 4, bi % 4
                    dst = accs[0:65, jb * 2048:(jb + 1) * 2048].rearrange("p (i rho) -> p rho i", rho=16)[:, 4 * a:4 * a + 4, :]
                    src = acc[0:65, 0:512].rearrange("p (rho i) -> p rho i", rho=4)
                if rr == 1:
                    tiles = [bi]
                elif rr == 4:
                    tiles = [bi]
                else:
                    tiles = [4 * (bi // 4) + t for t in range(4)]
                atoks = ["accs%d" % t for t in tiles]
                if g == GSEL[0]:
                    P.op("act", lambda e, dst=dst, src=src: e.copy(dst, src), r=[acctok], w=atoks)
                else:
                    P.op("dve", lambda e, dst=dst, src=src: e.tensor_tensor(dst, dst, src, ALU.add), r=[acctok] + atoks, w=atoks)
                if g == GSEL[-1] and rr == 16 and bi % 4 == 3:
                    for t in tiles:
                        finalize_tile(t)
        P.op("sp", lambda e: e.dma_start(out=D["x2in"][128:192, :], in_=ystage[:]),
             r=["yd%d" % T for T in range(16)], w=["x2in_d"], dma="st_yd")
        P.emit()


class LNCtx:
    def __init__(self, nc, es, pfx, g_ap, b_ap, ident):
        self.nc = nc
        self.pfx = pfx
        self.xt = Ring(nc, es, pfx + "_xt", [128, 1024], F32, 2)
        self.rt = Ring(nc, es, pfx + "_rt", [128, 1024], F32, 2)
        self.yo = Ring(nc, es, pfx + "_yo", [128, 1024], F32, 2)
        self.xb = Ring(nc, es, pfx + "_xb", [128, 1024], BF16, 2)
        self.st = Ring(nc, es, pfx + "_st", [128, 2, 6], F32, 2)
        self.mv = Ring(nc, es, pfx + "_mv", [128, 4], F32, 2)
        self.gbc = es.enter_context(nc.sbuf_tensor(pfx + "_gbc", [128, 1024], F32))
        self.bbc = es.enter_context(nc.sbuf_tensor(pfx + "_bbc", [128, 1024], F32))
        self.g_ap, self.b_ap, self.ident = g_ap, b_ap, ident
        self.epsb = es.enter_context(nc.sbuf_tensor(pfx + "_eps", [128, 1], F32))

    def load_consts(self, P):
        gsrc = bass.AP(self.g_ap.tensor, self.g_ap.offset, [[0, 128], [1, 1024]])
        bsrc = bass.AP(self.b_ap.tensor, self.b_ap.offset, [[0, 128], [1, 1024]])
        P.op("sp", lambda e: e.dma_start(out=self.gbc[:], in_=gsrc), w=[self.pfx + "gbc"], dma=self.pfx + "c_g")
        P.op("sp", lambda e: e.dma_start(out=self.bbc[:], in_=bsrc), w=[self.pfx + "bbc"], dma=self.pfx + "c_b")
        P.op("dve", lambda e: e.memset(self.epsb[:], LN_EPS), w=[self.pfx + "eps"])

    def tile(self, P, banks, btoks, x_src_ap, y_dst_ap, ydtok, xT_dst, xTtok, ps_ring):
        pfx = self.pfx
        xt, xttok = self.xt.next()
        P.op("sp", lambda e: e.dma_start(out=xt[:], in_=x_src_ap), w=[xttok], dma=xttok)
        rt, rttok = self.rt.next()
        for hf in range(2):
            P.op("dve", lambda e, hf=hf: e.scalar_tensor_tensor(
                out=rt[:, hf * 512:(hf + 1) * 512], in0=xt[:, hf * 512:(hf + 1) * 512], scalar=ALPHA,
                in1=banks[hf][:, :], op0=ALU.mult, op1=ALU.add), r=[xttok, btoks[hf]], w=[rttok + "h%d" % hf])
        st, sttok = self.st.next()
        mv, mvtok = self.mv.next()
        for hf in range(2):
            P.op("dve", lambda e, hf=hf: e.bn_stats(st[:, hf, :], rt[:, hf * 512:(hf + 1) * 512]),
                 r=[rttok + "h%d" % hf], w=[sttok + "h%d" % hf])
        P.op("dve", lambda e: e.bn_aggr(mv[:, 0:2], st[:].rearrange("p a b -> p (a b)")),
             r=[sttok + "h0", sttok + "h1"], w=[mvtok])
        P.op("act", lambda e: e.activation(out=mv[:, 2:3], in_=mv[:, 1:2], func=AF.Sqrt, bias=self.epsb[:, 0:1], scale=1.0),
             r=[mvtok, pfx + "eps"], w=[mvtok + "s"])
        P.op("dve", lambda e: e.reciprocal(mv[:, 2:3], mv[:, 2:3]), r=[mvtok + "s"], w=[mvtok + "r"])
        P.op("dve", lambda e: e.tensor_scalar(rt[:], rt[:], mv[:, 0:1], mv[:, 2:3], op0=ALU.subtract, op1=ALU.mult),
             r=[rttok + "h0", rttok + "h1", mvtok + "r"], w=[rttok + "n"])
        yo, yotok = self.yo.next()
        P.op("dve", lambda e: e.tensor_tensor(yo[:], rt[:], self.gbc[:], ALU.mult), r=[rttok + "n", pfx + "gbc"], w=[yotok + "m"])
        P.op("pool", lambda e: e.tensor_tensor(yo[:], yo[:], self.bbc[:], ALU.add), r=[yotok + "m", pfx + "bbc"], w=[yotok])
        P.op("sp", lambda e: e.dma_start(out=y_dst_ap, in_=yo[:]), r=[yotok], w=[ydtok], dma=yotok)
        xb, xbtok = self.xb.next()
        P.op("act", lambda e: e.copy(xb[:], yo[:]), r=[yotok], w=[xbtok])
        for q4 in range(2):
            tb, tbtok = ps_ring.next()
            for k4 in range(4):
                kc = q4 * 4 + k4
                P.op("pe", lambda e, tb=tb, k4=k4, kc=kc: e.matmul(tb[:, k4 * 128:(k4 + 1) * 128], xb[:, kc * 128:(kc + 1) * 128],
                                                                     self.ident[:], start=True, stop=True),
                     r=[xbtok, "ident"], w=[tbtok + "k%d" % k4])
            dst = xT_dst[:, q4 * 4:(q4 + 1) * 4, :]
            src = tb[:, :].rearrange("p (k t) -> p k t", k=4)
            P.op("act", lambda e, dst=dst, src=src: e.copy(dst, src), r=[tbtok + "k%d" % k for k in range(4)],
                 w=[xTtok + ("a" if q4 == 0 else "b")])


def phase_p3a(nc, P, D, layer, x_in_name):
    win = D["w_in"][layer].rearrange("(kc p) n -> p kc n", p=128)
    with contextlib.ExitStack() as es0:
        xT = es0.enter_context(nc.sbuf_tensor("p3_xT", [128, 8, TOK], BF16))
        ycT = es0.enter_context(nc.sbuf_tensor("p3_ycT", [128, 4, TOK], BF16))
        mergedT = es0.enter_context(nc.sbuf_tensor("p3_mT", [128, 8, TOK], BF16))
        sidx = es0.enter_context(nc.sbuf_tensor("sb_idx3", [128, NIDX], I32))
        ident = es0.enter_context(nc.sbuf_tensor("sb_ident3", [128, 128], BF16))
        ps = Ring(nc, es0, "p3_ps", [128, 512], F32, 8, psum=True)

        def gather(dst, table, col, p0, npart, wtok, chan):
            P.op("pool", lambda e: e.indirect_dma_start(
                out=dst, out_offset=None, in_=table,
                in_offset=bass.IndirectOffsetOnAxis(ap=sidx[p0:p0 + npart, col:col + 1], axis=0)),
                r=["idx"], w=[wtok], dma=chan)

        with contextlib.ExitStack() as es:
            u = es.enter_context(nc.sbuf_tensor("p3_u", [128, 4, TOK + 16], F32))
            uh = es.enter_context(nc.sbuf_tensor("p3_uh", [128, 8], F32))
            wsc = es.enter_context(nc.sbuf_tensor("p3_wsc", [128, 4, 3], F32))
            cv = Ring(nc, es, "p3_cv", [128, TOK], F32, 2)
            ws = WStream(nc, es, "p3_wb", [128, 8, 512], nstage=1, nbf=1)
            P.op("sp", lambda e: e.dma_start(out=sidx[:], in_=D["idx"][:, :]), w=["idx"], dma="c_idx")
            P.op("sp", lambda e: e.dma_start(out=ident[:], in_=D["ident"][:, :]), w=["ident"], dma="c_ident")
            P.op("sp", lambda e: e.dma_start(out=wsc[:], in_=D["wsc"][layer]), w=["wsc"], dma="c_wsc")
            for kc in range(8):
                P.op("sp", lambda e, kc=kc: e.dma_start(out=xT[:, kc, :], in_=D["xT_a"][kc * 128:(kc + 1) * 128, :]),
                     w=["xT%d" % kc], dma="ld_xT%d" % kc)
            xtoks = ["xT%d" % kc for kc in range(8)]
            gather(uh[:], D["uh_g"][:, :], IDX_HALO, 0, 128, "uh", "ld_uh")
            for cc in range(4):
                P.op("sp", lambda e, cc=cc: e.dma_start(out=u[:, cc, 16:TOK + 16], in_=D["u_scr"][cc * 128:(cc + 1) * 128, :]),
                     w=["u%d" % cc], dma="ld_u%d" % cc)
                P.op("dve", lambda e, cc=cc: e.tensor_copy(u[:, cc, 14:16], uh[:, cc * 2:cc * 2 + 2]), r=["uh"], w=["uhalo%d" % cc])
            wb, wbtok = ws.load(P, win[:, :, 3840:4352])
            for cc in range(4):
                c, ctok = cv.next()
                ur = ["u%d" % cc, "uhalo%d" % cc, "wsc"]
                P.op("dve", lambda e, c=c, cc=cc: e.tensor_scalar(c[:], u[:, cc, 16:TOK + 16], wsc[:, cc, 0:1], None, op0=ALU.mult),
                     r=ur, w=[ctok + "a"])
                P.op("dve", lambda e, c=c, cc=cc: e.scalar_tensor_tensor(out=c[:], in0=u[:, cc, 15:TOK + 15], scalar=wsc[:, cc, 1:2],
                                                                       in1=c[:], op0=ALU.mult, op1=ALU.add),
                     r=ur + [ctok + "a"], w=[ctok + "b"])
                P.op("dve", lambda e, c=c, cc=cc: e.scalar_tensor_tensor(out=c[:], in0=u[:, cc, 14:TOK + 14], scalar=wsc[:, cc, 2:3],
                                                                       in1=c[:], op0=ALU.mult, op1=ALU.add),
                     r=ur + [ctok + "b"], w=[ctok])
                for tt in range(4):
                    bank, btok = ps.next()
                    mm_group(P, bank[:, :], [(wb[:, kc, cc * 128:(cc + 1) * 128], xT[:, kc, tt * 512:(tt + 1) * 512]) for kc in range(8)],
                             r=[wbtok] + xtoks, w=[btok])
                    P.op("dve", lambda e, c=c, cc=cc, tt=tt, bank=bank: e.tensor_tensor(
                        ycT[:, cc, tt * 512:(tt + 1) * 512], c[:, tt * 512:(tt + 1) * 512], bank[:, :], ALU.mult),
                        r=[ctok, btok], w=["yc%d_%d" % (cc, tt)])
            P.emit()

        import os
        STOP = int(os.environ.get("P3A_STOP", "9"))
        if STOP <= 1:
            return
        with contextlib.ExitStack() as es:
            ymT = es.enter_context(nc.sbuf_tensor("p3_ymT", [128, 4, TOK], BF16))
            ydT = es.enter_context(nc.sbuf_tensor("p3_ydT", [64, 4, TOK], BF16))
            wst = Ring(nc, es, "p3_wst", [128, 36, 128], F32, 2)
            wbf = Ring(nc, es, "p3_wbf", [128, 36, 128], BF16, 2)
            sg = Ring(nc, es, "p3_sg", [128, 512], F32, 6)
            mt = Ring(nc, es, "p3_mt", [128, 512], F32, 6)
            x2v = D["x2g"].rearrange("n (i t) -> (n i) t", i=4)
            for r in range(4):
                gather(ymT[:, r, :], x2v, IDX_YM + r, 0, 128, "ym%d" % r, "ld_ym%d" % r)
                gather(ydT[0:64, r, :], x2v, IDX_YD + r, 0, 64, "yd%d" % r, "ld_yd%d" % r)
            ytoks = ["ym%d" % r for r in range(4)] + ["yd%d" % r for r in range(4)]
            wmp = D["w_moba_proj"][layer].rearrange("(kc p) n -> p kc n", p=128)
            wcp = D["w_conv_proj"][layer].rearrange("(kc p) n -> p kc n", p=128)
            wdp = D["w_dil_proj"][layer].rearrange("(kc p) n -> p kc n", p=64)
            for oc in range(8):
                st, sttok = wst.next()
                bf, bftok = wbf.next()
                cs = slice(oc * 128, (oc + 1) * 128)
                srcs = [(0, 4, 128, wmp[:, :, cs]), (4, 8, 128, wcp[:, :, cs]), (8, 12, 64, wdp[:, :, cs])]
                for b in range(3):
                    c0 = 5376 + b * 1024 + oc * 128
                    srcs.append((12 + 8 * b, 20 + 8 * b, 128, win[:, :, c0:c0 + 128]))
                for si, (a0, a1, npart, src) in enumerate(srcs):
                    P.op("sp", lambda e, st=st, a0=a0, a1=a1, npart=npart, src=src: e.dma_start(out=st[0:npart, a0:a1, :], in_=src),
                         w=[sttok + "s%d" % si], dma=sttok + "s%d" % si)
                if oc < 2:
                    P.op("pool", lambda e, st=st: e.memset(st[64:128, 8:12, :], 0.0), w=[sttok + "z"])
                if oc % 2 == 0:
                    P.op("act", lambda e, st=st, bf=bf: e.copy(bf[:], st[:]),
                         r=[sttok + "s%d" % si for si in range(6)] + [sttok + "z"], w=[bftok])
                else:
                    P.op("dve", lambda e, st=st, bf=bf: e.tensor_copy(bf[:], st[:]),
                         r=[sttok + "s%d" % si for si in range(6)] + [sttok + "z"], w=[bftok])
                for tt in range(4):
                    ts = slice(tt * 512, (tt + 1) * 512)
                    gb = []
                    for b in range(3):
                        bank, btok = ps.next()
                        mm_group(P, bank[:, :], [(bf[:, 12 + 8 * b + kc, :], xT[:, kc, ts]) for kc in range(8)], r=[bftok], w=[btok])
                        s_, stok_ = sg.next()
                        P.op("act", lambda e, s_=s_, bank=bank: e.activation(out=s_[:], in_=bank[:, :], func=AF.Sigmoid), r=[btok], w=[stok_])
                        gb.append((s_, stok_))
                    pb = []
                    for (a0, n, npart, src, srct) in [(0, 4, 128, ymT, ytoks), (4, 4, 128, ycT, []), (8, 4, 64, ydT, ytoks)]:
                        bank, btok = ps.next()
                        mm_group(P, bank[:, :], [(bf[0:npart, a0 + kc, :], src[0:npart, kc, ts]) for kc in range(n)], r=[bftok] + srct, w=[btok])
                        pb.append((bank, btok))
                    ms = []
                    for b in range(3):
                        m_, mtok_ = mt.next()
                        P.op("dve", lambda e, m_=m_, b=b, gb=gb, pb=pb: e.tensor_tensor(m_[:], gb[b][0][:], pb[b][0][:, :], ALU.mult),
                             r=[gb[b][1], pb[b][1]], w=[mtok_])
                        ms.append((m_, mtok_))
                    P.op("pool", lambda e, ms=ms: e.tensor_tensor(ms[0][0][:], ms[0][0][:], ms[1][0][:], ALU.add),
                         r=[ms[0][1], ms[1][1]], w=[ms[0][1]])
                    P.op("pool", lambda e, ms=ms, oc=oc, ts=ts: e.tensor_tensor(mergedT[:, oc, ts], ms[0][0][:], ms[2][0][:], ALU.add),
                         r=[ms[0][1], ms[2][1]], w=["mT%d_%d" % (oc, tt)])
            P.emit()

        if "dbg_m" in D:
            for kc in range(8):
                P.op("sp", lambda e, kc=kc: e.dma_start(out=D["dbg_m"][kc * 128:(kc + 1) * 128, :], in_=mergedT[:, kc, :]), w=["dbgm%d" % kc], dma="dbgm")
            for kc in range(4):
                P.op("sp", lambda e, kc=kc: e.dma_start(out=D["dbg_yc"][kc * 128:(kc + 1) * 128, :], in_=ycT[:, kc, :]), w=["dbgy%d" % kc], dma="dbgy")
            P.emit()
        if STOP <= 2:
            return
        with contextlib.ExitStack() as es:
            x1T = es.enter_context(nc.sbuf_tensor("p3_x1T", [128, 8, TOK], BF16))
            wmo = es.enter_context(nc.sbuf_tensor("p3_wmo", [128, 8, 1024], BF16))
            ws = WStream(nc, es, "p3_wmo_s", [128, 8, 512], nstage=1, nbf=1)
            ln = LNCtx(nc, es, "ln1", D["ln1_g"][layer], D["ln1_b"][layer], ident)
            ln.load_consts(P)
            wmov = D["w_mix_out"][layer].rearrange("(kc p) n -> p kc n", p=128)
            for hf in range(2):
                wb, wbtok = ws.load(P, wmov[:, :, hf * 512:(hf + 1) * 512])
                P.op("pool", lambda e, wb=wb, hf=hf: e.tensor_copy(wmo[:, :, hf * 512:(hf + 1) * 512], wb[:]), r=[wbtok], w=["wmo%d" % hf])
            def mix_mm(t16):
                tsl = slice(t16 * 128, (t16 + 1) * 128)
                banks, btoks = [], []
                for hf in range(2):
                    bank, btok = ps.next()
                    mm_group(P, bank[:, :], [(mergedT[:, kc, tsl], wmo[:, kc, hf * 512:(hf + 1) * 512]) for kc in range(8)],
                             r=["wmo%d" % hf], w=[btok])
                    banks.append(bank)
                    btoks.append(btok)
                return banks, btoks

            nxt = mix_mm(0)
            for t16 in range(16):
                tsl = slice(t16 * 128, (t16 + 1) * 128)
                banks, btoks = nxt
                if t16 + 1 < 16:
                    nxt = mix_mm(t16 + 1)
                ln.tile(P, banks, btoks, D[x_in_name][tsl, :], D["xres_b"][tsl, :], "xres_b%d" % t16,
                        x1T[:, :, tsl], "x1T%d" % t16, ps)
            alltoks = ["x1T%d%s" % (t, s) for t in range(16) for s in "ab"]
            for kc in range(8):
                P.op("sp" if kc % 2 else "pool", lambda e, kc=kc: e.dma_start(out=D["xT_b"][kc * 128:(kc + 1) * 128, :], in_=x1T[:, kc, :]),
                     r=alltoks, w=["xT_b%d" % kc], dma="st_x1T%d" % kc)
            xz = es.enter_context(nc.sbuf_tensor("p3_xz", [128, 16], BF16))
            P.op("pool", lambda e: e.memset(xz[:], 0.0), w=["xz"])
            P.op("sp", lambda e: e.dma_start(out=D["x3in"][0:128, :].rearrange("p (k t) -> p k t", k=8), in_=x1T[:, :, TOK - 2:TOK]),
                 r=["x1T15a", "x1T15b"], w=["x3in"], dma="st_x3")
            P.op("sp", lambda e: e.dma_start(out=D["x3in"][128:256, :], in_=xz[:]), r=["xz"], w=["x3inz"], dma="st_x3")
            P.emit()


def host_small_layouts(inputs):
    wsc = np.ascontiguousarray(np.asarray(inputs["w_short_conv"], np.float32).reshape(DEPTH, 3, 4, 128).transpose(0, 3, 2, 1))
    wfc = np.ascontiguousarray(np.asarray(inputs["w_ffn_conv"], np.float32).reshape(DEPTH, 3, 44, 128).transpose(0, 3, 2, 1))
    bfc = np.ascontiguousarray(np.asarray(inputs["b_ffn_conv"], np.float32).reshape(DEPTH, 44, 128).transpose(0, 2, 1))
    return wsc, wfc, bfc


def phase_p3b(nc, P, D, layer, out_name, write_xT):
    wup = D["w_up"][layer].rearrange("(kc p) n -> p kc n", p=128)
    wdn = D["w_down"][layer].rearrange("(i p) n -> p i n", p=128)
    xTb = D["xT_b"].rearrange("(kc p) t -> p kc t", p=128)
    NI = D_FF // 128
    with contextlib.ExitStack() as es0:
        hT = es0.enter_context(nc.sbuf_tensor("f_hT", [128, NI, TOK], BF16))
        sidx = es0.enter_context(nc.sbuf_tensor("sb_idx4", [128, NIDX], I32))
        ident = es0.enter_context(nc.sbuf_tensor("sb_ident4", [128, 128], BF16))
        ps = Ring(nc, es0, "f_ps", [128, 512], F32, 7, psum=True)
        psh = Ring(nc, es0, "f_psh", [128, 512], F32, 1, psum=True)
        with contextlib.ExitStack() as es:
            xT = es.enter_context(nc.sbuf_tensor("f_xT", [128, 8, 16 + TOK], BF16))
            wfc = es.enter_context(nc.sbuf_tensor("f_wfc", [128, 44, 3], F32))
            bfc = es.enter_context(nc.sbuf_tensor("f_bfc", [128, 44], F32))
            xh = es.enter_context(nc.sbuf_tensor("f_xh", [128, 16], BF16))
            wst = Ring(nc, es, "f_wst", [128, 8, 256], F32, 2)
            wbf = Ring(nc, es, "f_wbf", [128, 8, 256], BF16, 3)
            ug = Ring(nc, es, "f_ug", [128, 16 + TOK], F32, 2)
            uv = Ring(nc, es, "f_uv", [128, 16 + TOK], F32, 2)
            ag = Ring(nc, es, "f_ag", [128, TOK], F32, 1)
            av = Ring(nc, es, "f_av", [128, TOK], F32, 1)
            P.op("sp", lambda e: e.dma_start(out=sidx[:], in_=D["idx"][:, :]), w=["idx"], dma="c_idx")
            P.op("sp", lambda e: e.dma_start(out=ident[:], in_=D["ident"][:, :]), w=["ident"], dma="c_ident")
            P.op("sp", lambda e: e.dma_start(out=wfc[:], in_=D["wfc"][layer]), w=["wfc"], dma="c_wfc")
            P.op("sp", lambda e: e.dma_start(out=bfc[:], in_=D["bfc"][layer]), w=["bfc"], dma="c_bfc")
            P.op("pool", lambda e: e.indirect_dma_start(
                out=xh[:], out_offset=None, in_=D["x3g"][:, :],
                in_offset=bass.IndirectOffsetOnAxis(ap=sidx[0:128, IDX_HALO:IDX_HALO + 1], axis=0)),
                r=["idx"], w=["xh"], dma="ld_xh")
            for kc in range(8):
                P.op("sp", lambda e, kc=kc: e.dma_start(out=xT[:, kc, 16:16 + TOK], in_=D["xT_b"][kc * 128:(kc + 1) * 128, :]),
                     w=["xT%d" % kc], dma="ld_xT%d" % kc)
            P.op("dve", lambda e: e.tensor_copy(xT[:, :, 14:16], xh[:].rearrange("p (k t) -> p k t", k=8)), r=["xh"], w=["xTh"])
            xr = ["xT%d" % kc for kc in range(8)]
            wq = {}

            def prefetch_w(i):
                st, sttok = wst.next()
                wb, wbtok = wbf.next()
                P.op("sp", lambda e, st=st, i=i: e.dma_start(out=st[:, :, 0:128], in_=wup[:, :, i * 128:(i + 1) * 128]), w=[sttok + "g"], dma=sttok + "g")
                P.op("sp", lambda e, st=st, i=i: e.dma_start(out=st[:, :, 128:256], in_=wup[:, :, D_FF + i * 128:D_FF + (i + 1) * 128]), w=[sttok + "v"], dma=sttok + "v")
                if i % 2 == 0:
                    P.op("act", lambda e, st=st, wb=wb: e.copy(wb[:], st[:]), r=[sttok + "g", sttok + "v"], w=[wbtok])
                else:
                    P.op("dve", lambda e, st=st, wb=wb: e.tensor_copy(wb[:], st[:]), r=[sttok + "g", sttok + "v"], w=[wbtok])
                wq[i] = (wb, wbtok)

            prefetch_w(0)
            prefetch_w(1)
            for i in range(NI):
                if i + 2 < NI:
                    prefetch_w(i + 2)
                wb, wbtok = wq.pop(i)
                u_g, ugtok = ug.next()
                u_v, uvtok = uv.next()
                H, htok = psh.next()
                mm_group(P, H[:, 0:2], [(wb[:, kc, 0:128], xT[:, kc, 14:16]) for kc in range(8)], r=[wbtok, "xTh"], w=[htok + "g"])
                mm_group(P, H[:, 2:4], [(wb[:, kc, 128:256], xT[:, kc, 14:16]) for kc in range(8)], r=[wbtok, "xTh"], w=[htok + "v"])
                P.op("dve", lambda e, u_g=u_g, H=H: e.tensor_copy(u_g[:, 14:16], H[:, 0:2]), r=[htok + "g"], w=[ugtok + "h"])
                P.op("dve", lambda e, u_v=u_v, H=H: e.tensor_copy(u_v[:, 14:16], H[:, 2:4]), r=[htok + "v"], w=[uvtok + "h"])
                for tg in range(4):
                    cs = slice(16 + tg * 512, 16 + (tg + 1) * 512)
                    G, gtok = ps.next()
                    V, vtok = ps.next()
                    mm_group(P, G[:, :], [(wb[:, kc, 0:128], xT[:, kc, cs]) for kc in range(8)], r=[wbtok] + xr, w=[gtok])
                    mm_group(P, V[:, :], [(wb[:, kc, 128:256], xT[:, kc, cs]) for kc in range(8)], r=[wbtok] + xr, w=[vtok])
                    P.op("act", lambda e, u_g=u_g, G=G, cs=cs: e.copy(u_g[:, cs], G[:, :]), r=[gtok], w=[ugtok + "d%d" % tg])
                    P.op("act", lambda e, u_v=u_v, V=V, cs=cs: e.copy(u_v[:, cs], V[:, :]), r=[vtok], w=[uvtok + "d%d" % tg])
                outs = []
                for (u_, utok, aring, ci) in [(u_g, ugtok, ag, i), (u_v, uvtok, av, NI + i)]:
                    a_, atok = aring.next()
                    ur = [utok + "d%d" % t for t in range(4)]
                    P.op("act", lambda e, u_=u_, a_=a_, ci=ci: e.activation(out=a_[:], in_=u_[:, 16:16 + TOK], func=AF.Identity,
                                                                             bias=bfc[:, ci:ci + 1], scale=wfc[:, ci, 0:1]),
                         r=ur + ["wfc", "bfc"], w=[atok + "0"])
                    P.op("dve", lambda e, u_=u_, a_=a_, ci=ci: e.scalar_tensor_tensor(out=a_[:], in0=u_[:, 15:15 + TOK], scalar=wfc[:, ci, 1:2],
                                                                                     in1=a_[:], op0=ALU.mult, op1=ALU.add),
                         r=ur + [utok + "h", atok + "0"], w=[atok + "1"])
                    P.op("dve", lambda e, u_=u_, a_=a_, ci=ci: e.scalar_tensor_tensor(out=a_[:], in0=u_[:, 14:14 + TOK], scalar=wfc[:, ci, 2:3],
                                                                                     in1=a_[:], op0=ALU.mult, op1=ALU.add),
                         r=ur + [utok + "h", atok + "1"], w=[atok])
                    outs.append((a_, atok))
                P.op("act", lambda e, a_=outs[0][0]: e.activation(out=a_[:], in_=a_[:], func=AF.Silu), r=[outs[0][1]], w=[outs[0][1] + "s"])
                P.op("pool", lambda e, s_=outs[0][0], a_=outs[1][0], i=i: e.tensor_tensor(hT[:, i, :], s_[:], a_[:], ALU.mult),
                     r=[outs[0][1] + "s", outs[1][1]], w=["hT%d" % i])
            P.emit()
        with contextlib.ExitStack() as es:
            wdnb = es.enter_context(nc.sbuf_tensor("f_wdnb", [128, NI, 1024], BF16))
            dst_ = Ring(nc, es, "f_dst", [128, 1024], F32, 2)
            x2T = Ring(nc, es, "f_x2T", [128, 8, 512], BF16, 2)
            ln = LNCtx(nc, es, "ln2", D["ln2_g"][layer], D["ln2_b"][layer], ident)
            ln.load_consts(P)
            for i in range(NI):
                st, sttok = dst_.next()
                P.op("sp", lambda e, st=st, i=i: e.dma_start(out=st[:], in_=wdn[:, i, :]), w=[sttok], dma=sttok)
                if i % 2 == 0:
                    P.op("act", lambda e, st=st, i=i: e.copy(wdnb[:, i, :], st[:]), r=[sttok], w=["wdnb%d" % i])
                else:
                    P.op("dve", lambda e, st=st, i=i: e.tensor_copy(wdnb[:, i, :], st[:]), r=[sttok], w=["wdnb%d" % i])
            def down_mm(t16):
                tsl = slice(t16 * 128, (t16 + 1) * 128)
                banks, btoks = [], []
                for hf in range(2):
                    bank, btok = ps.next()
                    mm_group(P, bank[:, :], [(hT[:, i, tsl], wdnb[:, i, hf * 512:(hf + 1) * 512]) for i in range(NI)],
                             r=[], w=[btok], rk=[["wdnb%d" % i] for i in range(NI)])
                    banks.append(bank)
                    btoks.append(btok)
                return banks, btoks

            nxt = down_mm(0)
            for tg in range(4):
                xo, xotok = x2T.next()
                for tq in range(4):
                    t16 = tg * 4 + tq
                    tsl = slice(t16 * 128, (t16 + 1) * 128)
                    banks, btoks = nxt
                    if t16 + 1 < 16:
                        nxt = down_mm(t16 + 1)
                    ln.tile(P, banks, btoks, D["xres_b"][tsl, :], D[out_name][tsl, :], "yout%d" % t16,
                            xo[:, :, tq * 128:(tq + 1) * 128], xotok + "_%d" % tq, ps)
                if write_xT:
                    P.op("sp", lambda e, xo=xo, tg=tg: e.dma_start(out=D["xT_a"].rearrange("(kc p) t -> p kc t", p=128)[:, :, tg * 512:(tg + 1) * 512], in_=xo[:]),
                         r=[xotok + "_%d%s" % (tq, s_) for tq in range(4) for s_ in "ab"], w=["xT_a_%d" % tg], dma=xotok)
            P.emit()


GROUPS = [[0, 1, 2, 3], [4, 5, 6, 7]]
W_P3A = ["w_moba_proj", "w_dil_proj", "w_conv_proj", "w_mix_out", "ln1_g", "ln1_b"]
W_P3B = ["w_up", "w_down", "ln2_g", "ln2_b"]
_PROG_CACHE = {}


def _prog(key, builder):
    if key not in _PROG_CACHE:
        _PROG_CACHE[key] = builder()
    return _PROG_CACHE[key]


def _build_p1(first):
    nc = bass.Bass("TRN2", target_bir_lowering=False)
    D = make_D(nc, ["w_in", "xT_f32" if first else "xT_a"], (["xT_a"] if first else []) + ["x1in", "u_scr", "uh_in"], layers=1)
    phase_p1(nc, Prog(nc), D, 0, first)
    return nc


def _build_p2():
    nc = bass.Bass("TRN2", target_bir_lowering=False)
    D = make_D(nc, ["x1g", "idx", "blkind", "gtab", "dmask", "ident", "dilmask"], ["x2in"])
    P = Prog(nc)
    phase_p2a(nc, P, D)
    phase_p2b(nc, P, D)
    return nc


def _build_p3a():
    nc = bass.Bass("TRN2", target_bir_lowering=False)
    D = make_D(nc, ["w_in", "xT_a", "x_tm", "x2g", "u_scr", "uh_g", "idx", "ident", "wsc"] + W_P3A, ["xres_b", "xT_b", "x3in"], layers=1)
    phase_p3a(nc, Prog(nc), D, 0, "x_tm")
    return nc


def _build_p3b():
    nc = bass.Bass("TRN2", target_bir_lowering=False)
    D = make_D(nc, ["xT_b", "xres_b", "x3g", "idx", "ident", "wfc", "bfc"] + W_P3B, ["xres_a", "xT_a"], ["wup_bf"], layers=1)
    phase_p3b(nc, Prog(nc), D, 0, "xres_a", True)
    return nc


def _run(nc, in_maps):
    res = run_bass_kernel_spmd(nc, in_maps, core_ids=list(range(8)))
    return res.results


def _gather_groups(per_core, ch=None):
    out = []
    n = per_core[0].shape[0]
    ch = ch or n
    for c in range(8):
        g = c // 4
        out.append(np.concatenate([per_core[g * 4 + r][k:k + ch] for k in range(0, n, ch) for r in range(4)], 0))
    return out


def kernel_unfused(inputs):
    x = np.asarray(inputs["x"], np.float32)
    wsc, wfc, bfc = host_small_layouts(inputs)
    C = host_consts()
    idx = [host_idx(c) for c in range(8)]
    W = {k: np.asarray(v, np.float32) for k, v in inputs.items()}
    xs = [np.ascontiguousarray(x[c // 4, (c % 4) * TOK:(c % 4 + 1) * TOK]) for c in range(8)]
    x_tm = xs
    xT_a = None
    for l in range(DEPTH):
        sl = slice(l, l + 1)
        if l == 0:
            r = _run(_prog("p1f", lambda: _build_p1(True)),
                     [{"w_in": W["w_in"][sl], "xT_f32": np.ascontiguousarray(xs[c].T)} for c in range(8)])
            xT_a = [r[c]["xT_a"] for c in range(8)]
        else:
            r = _run(_prog("p1n", lambda: _build_p1(False)), [{"w_in": W["w_in"][sl], "xT_a": xT_a[c]} for c in range(8)])
        x1g = _gather_groups([r[c]["x1in"] for c in range(8)], CH1)
        uh_g = _gather_groups([r[c]["uh_in"] for c in range(8)])
        u_scr = [r[c]["u_scr"] for c in range(8)]
        r = _run(_prog("p2", _build_p2),
                 [{"x1g": x1g[c], "idx": idx[c], "blkind": C["blkind"], "gtab": C["gtab"], "dmask": C["dmask"],
                   "ident": C["ident"], "dilmask": C["dilmask"]} for c in range(8)])
        x2g = _gather_groups([r[c]["x2in"] for c in range(8)], CH2)
        maps = []
        for c in range(8):
            m = {"w_in": W["w_in"][sl], "xT_a": xT_a[c], "x_tm": x_tm[c], "x2g": x2g[c], "u_scr": u_scr[c], "uh_g": uh_g[c],
                 "idx": idx[c], "ident": C["ident"], "wsc": wsc[sl]}
            for nm in W_P3A:
                m[nm] = W[nm][sl]
            maps.append(m)
        r = _run(_prog("p3a", _build_p3a), maps)
        x3g = _gather_groups([r[c]["x3in"] for c in range(8)])
        maps = []
        for c in range(8):
            m = {"xT_b": r[c]["xT_b"], "xres_b": r[c]["xres_b"], "x3g": x3g[c], "idx": idx[c], "ident": C["ident"],
                 "wfc": wfc[sl], "bfc": bfc[sl]}
            for nm in W_P3B:
                m[nm] = W[nm][sl]
            maps.append(m)
        r = _run(_prog("p3b", _build_p3b), maps)
        x_tm = [r[c]["xres_a"] for c in range(8)]
        xT_a = [r[c]["xT_a"] for c in range(8)]
    out = np.zeros((2, SEQ, D_MODEL), np.float32)
    for c in range(8):
        out[c // 4, (c % 4) * TOK:(c % 4 + 1) * TOK] = x_tm[c]
    return out


class NCX:
    _uid = [0]

    def __init__(self, nc):
        object.__setattr__(self, "_nc", nc)
        NCX._uid[0] += 1
        object.__setattr__(self, "_sfx", "_u%d" % NCX._uid[0])

    def __getattr__(self, k):
        return getattr(self._nc, k)

    def sbuf_tensor(self, name, *a, **kw):
        return self._nc.sbuf_tensor(name + self._sfx, *a, **kw)

    def psum_tensor(self, name, *a, **kw):
        return self._nc.psum_tensor(name + self._sfx, *a, **kw)


FUSED_IN = ["x_tm", "xT_f32", "w_in", "idx", "blkind", "gtab", "dmask", "ident", "dilmask", "wsc", "wfc", "bfc"] + W_P3A + W_P3B
FUSED_INT = ["xT_a", "xT_b", "x1in", "x1g", "u_scr", "uh_in", "uh_g", "x2in", "x2g", "xres_a", "xres_b", "x3in", "x3g"]


def _build_fused():
    nc = bass.Bass("TRN2", target_bir_lowering=False)
    D = make_D(nc, FUSED_IN, ["out"], FUSED_INT, layers=DEPTH)
    P = Prog(nc)

    import os
    NOCC = os.environ.get("NOCC", "")

    def allgather(src, dst, tag, ch=None, order=None):
        if NOCC == "all" or tag in NOCC.split(","):
            return
        if ch is not None:
            n = D[src].shape[0]
            for k in (order or range(n // ch)):
                P.op("pool", lambda e, k=k: e.collective_compute(
                    "AllGather", ALU.bypass, replica_groups=GROUPS,
                    ins=[D[src][k * ch:(k + 1) * ch, :]], outs=[D[dst][k * 4 * ch:(k + 1) * 4 * ch, :]]),
                    w=["%s_c%d" % (dst, k)], cc="%s_%d" % (tag, k))
            return
        P.op("pool", lambda e: e.collective_compute("AllGather", ALU.bypass, replica_groups=GROUPS,
                                                    ins=[D[src]], outs=[D[dst]]), cc=tag)

    for l in range(DEPTH):
        last = l == DEPTH - 1
        phase_p1(NCX(nc), P, D, l, l == 0,
                 mid_hook=lambda: (allgather("x1in", "x1g", "ag_x1", CH1, order=list(range(10))), allgather("uh_in", "uh_g", "ag_uh")))
        allgather("x1in", "x1g", "ag_x1b", CH1, order=[10, 11, 12, 13, 14, 15])
        phase_p2a(NCX(nc), P, D)
        phase_p2b(NCX(nc), P, D)
        allgather("x2in", "x2g", "ag_x2", CH2)
        P.emit()
        phase_p3a(NCX(nc), P, D, l, "x_tm" if l == 0 else "xres_a")
        allgather("x3in", "x3g", "ag_x3")
        P.emit()
        phase_p3b(NCX(nc), P, D, l, "out" if last else "xres_a", not last)
    return nc


def kernel_fused(inputs):
    x = np.asarray(inputs["x"], np.float32)
    wsc, wfc, bfc = host_small_layouts(inputs)
    C = host_consts()
    W = {k: np.asarray(v, np.float32) for k, v in inputs.items()}
    maps = []
    for c in range(8):
        xc = np.ascontiguousarray(x[c // 4, (c % 4) * TOK:(c % 4 + 1) * TOK])
        m = {"x_tm": xc, "xT_f32": np.ascontiguousarray(xc.T), "w_in": W["w_in"], "idx": host_idx(c),
             "blkind": C["blkind"], "gtab": C["gtab"], "dmask": C["dmask"], "ident": C["ident"], "dilmask": C["dilmask"],
             "wsc": wsc, "wfc": wfc, "bfc": bfc}
        for nm in W_P3A + W_P3B:
            m[nm] = W[nm]
        maps.append(m)
    r = _run(_prog("fused", _build_fused), maps)
    out = np.zeros((2, SEQ, D_MODEL), np.float32)
    for c in range(8):
        out[c // 4, (c % 4) * TOK:(c % 4 + 1) * TOK] = r[c]["out"]
    return out


def kernel(**inputs):
    return kernel_fused(inputs)
```

```python
import contextlib
import numpy as np
import concourse.bass as bass
import concourse.mybir as mybir
from concourse.bass_utils import run_bass_kernel_spmd

F32 = mybir.dt.float32
BF16 = mybir.dt.bfloat16
I32 = mybir.dt.int32
AF = mybir.ActivationFunctionType
ALU = mybir.AluOpType
AX = mybir.AxisListType

D_MODEL = 1024
SEQ = 8192
DEPTH = 2
TOK = 2048
IN_COLS = 8448
D_FF = 2816
ALPHA = (2 * DEPTH) ** 0.25
LN_EPS = 1e-5
NEG = -30000.0

R_QA, R_KA, R_QD, R_KD, R_VM, R_VD, R_X1 = 0, 512, 1024, 1792, 2560, 3072, 4096
NIDX = 64
IDX_K, IDX_Q, IDX_V = 0, 8, 16
IDX_DQ, IDX_DK, IDX_DV = 20, 32, 44
IDX_YM, IDX_YD, IDX_HALO = 52, 56, 60


CH1, CH2 = 256, 64


def gx1(r, row):
    return (row // CH1) * (4 * CH1) + r * CH1 + (row % CH1)


def gx2(r, row):
    return (row // CH2) * (4 * CH2) + r * CH2 + (row % CH2)


def host_idx(c):
    j = c % 4
    p = np.arange(128)
    T = np.zeros((128, NIDX), np.int32)
    for h in range(2):
        for r in range(4):
            T[:, IDX_K + h * 4 + r] = gx1(r, R_KA + 128 * j + 64 * h + p)
            T[:, IDX_Q + h * 4 + r] = gx1(r, R_QA + 128 * j + 64 * h + p)
    for r in range(4):
        T[:, IDX_V + r] = gx1(r, R_VM + 128 * j + p)
    for g in range(3):
        hd = 4 * g + j
        for r in range(4):
            T[:, IDX_DQ + g * 4 + r] = gx1(r, R_QD + 64 * hd + p)
            T[:, IDX_DK + g * 4 + r] = gx1(r, R_KD + 64 * hd + p)
    for r in range(4):
        for hf in range(2):
            row = gx1(r, R_VD + 256 * j + 2 * p + hf)
            T[:, IDX_DV + r * 2 + hf] = row if hf == 0 else 2 * row
    for r in range(4):
        T[:, IDX_YM + r] = gx2(r, p) * 4 + j
        T[:, IDX_YD + r] = gx2(r, 128 + (p % 64)) * 4 + j
    T[:, IDX_HALO] = ((j - 1) * 256 + p) if j > 0 else (128 + p)
    return T


class Tok:
    __slots__ = ("name", "writer", "readers")

    def __init__(self, name=""):
        self.name = name
        self.writer = None
        self.readers = []


class Op:
    __slots__ = ("eng", "fn", "deps", "need_inc", "semkey", "semval", "kind", "name")


class Prog:
    ENGS = ("pe", "act", "dve", "pool", "sp")

    def __init__(self, nc, info_ap=None):
        self.nc = nc
        self.ops = []
        self.toks = {}
        self.sems = {}
        self.counts = {}
        self.info_ap = info_ap
        self.dyn = {}

    def tok(self, name):
        t = self.toks.get(name)
        if t is None:
            t = Tok(name)
            self.toks[name] = t
        return t

    def _toks(self, xs):
        out = []
        if isinstance(xs, (str, Tok)):
            xs = [xs]
        for x in xs or ():
            if isinstance(x, Tok):
                out.append(x)
            elif isinstance(x, (list, tuple)):
                out.extend(self._toks(x))
            else:
                out.append(self.tok(x))
        return out

    def op(self, eng, fn, r=(), w=(), dma=None, cc=None, name=""):
        o = Op()
        o.eng = eng
        o.fn = fn
        o.kind = "dma" if dma is not None else ("cc" if cc is not None else "eng")
        o.semkey = ("dma", dma) if dma is not None else (("cc", cc) if cc is not None else ("eng", eng))
        o.need_inc = o.kind != "eng"
        o.semval = None
        o.name = name
        deps = []
        rt = self._toks(r)
        wt = self._toks(w)
        for t in rt:
            if t.writer is not None:
                deps.append(t.writer)
        for t in wt:
            if t.writer is not None:
                deps.append(t.writer)
            deps.extend(t.readers)
        for t in rt:
            t.readers.append(o)
        for t in wt:
            t.writer = o
            t.readers = []
        seen = set()
        dd = []
        for d in deps:
            if id(d) in seen or d is o:
                continue
            seen.add(id(d))
            if d.kind == "eng" and o.kind == "eng" and d.eng == "pe" and o.eng == "pe":
                continue
            dd.append(d)
        o.deps = dd
        for d in dd:
            d.need_inc = True
        self.ops.append(o)
        return o

    def emit(self):
        nc = self.nc
        if not hasattr(self, "totals"):
            self.totals = {}
            self.handles = {}
            self.pool = []
            self.eng_sem = {}
        prev_totals = dict(self.totals)
        for e in self.ENGS:
            last = [o for o in self.ops if o.eng == e and o.kind == "eng"]
            if last:
                last[-1].need_inc = True
        keymap = {}

        def sem_for(key):
            if key in keymap:
                return keymap[key]
            if key[0] == "eng":
                nm = self.eng_sem.get(key)
                if nm is None:
                    nm = "se_%s" % key[1]
                    self.handles[nm] = nc.alloc_semaphore(name=nm)
                    self.totals[nm] = 0
                    self.eng_sem[key] = nm
            else:
                if self.pool:
                    nm = self.pool.pop()
                else:
                    nm = "sd_%d" % len(self.handles)
                    self.handles[nm] = nc.alloc_semaphore(name=nm)
                    self.totals[nm] = 0
            keymap[key] = nm
            return nm

        for o in self.ops:
            if o.need_inc:
                nm = sem_for(o.semkey)
                step = 16 if o.kind == "dma" else 1
                self.totals[nm] += step
                o.semval = self.totals[nm]
                o.semkey = nm
        for o in self.ops:
            if not o.need_inc:
                o.semkey = None
        per_eng = {e: [o for o in self.ops if o.eng == e] for e in self.ENGS}
        totals = dict(self.totals)
        handles = self.handles
        with nc.Block() as block:

            def run(engname, engobj):
                waited = {}
                if per_eng[engname] or engname == "sp":
                    for k, v in prev_totals.items():
                        if v > 0:
                            engobj.wait_ge(handles[k], v)
                            waited[k] = v
                for o in per_eng[engname]:
                    for d in o.deps:
                        k = d.semkey
                        if waited.get(k, 0) >= d.semval:
                            continue
                        engobj.wait_ge(handles[k], d.semval)
                        waited[k] = d.semval
                    ins = o.fn(engobj)
                    if o.need_inc:
                        ins.then_inc(handles[o.semkey], 16 if o.kind == "dma" else 1)
                if engname == "sp":
                    for k, v in totals.items():
                        if waited.get(k, 0) < v:
                            engobj.wait_ge(handles[k], v)

            @block.tensor
            def _(e):
                run("pe", e)

            @block.scalar
            def _(e):
                run("act", e)

            @block.vector
            def _(e):
                run("dve", e)

            @block.gpsimd
            def _(e):
                run("pool", e)

            @block.sync
            def _(e):
                run("sp", e)
        for key, nm in keymap.items():
            if key[0] != "eng":
                self.pool.append(nm)
        self.ops = []
        self.toks = {}


class Ring:
    def __init__(self, nc, es, name, shape, dtype, n, psum=False):
        self.name = name
        self.n = n
        self.i = 0
        mk = nc.psum_tensor if psum else nc.sbuf_tensor
        self.tiles = [es.enter_context(mk("%s%d" % (name, k), list(shape), dtype)) for k in range(n)]

    def next(self):
        k = self.i % self.n
        self.i += 1
        return self.tiles[k], "%s#%d" % (self.name, k)


def mm_group(P, out_ap, pairs, r, w, rk=None):
    n = len(pairs)
    for k, (l, rr) in enumerate(pairs):
        P.op("pe", lambda e, l=l, rr=rr, k=k: e.matmul(out_ap, l, rr, start=(k == 0), stop=(k == n - 1)),
             r=list(r) + (rk[k] if rk is not None else []), w=w)


class WStream:
    def __init__(self, nc, es, name, shape, nstage=2, nbf=2):
        self.stage = Ring(nc, es, name + "_st", shape, F32, nstage)
        self.bf = Ring(nc, es, name + "_bf", shape, BF16, nbf)
        self.k = 0

    def load(self, P, src_ap, sub=None, queue="sp"):
        st, stok = self.stage.next()
        bf, btok = self.bf.next()
        sl = sub if sub is not None else (lambda t: t[:])
        P.op(queue, lambda e: e.dma_start(out=sl(st), in_=src_ap), w=[stok], dma=stok)
        eng = "act" if (self.k % 2 == 0) else "dve"
        self.k += 1
        if eng == "act":
            P.op("act", lambda e: e.copy(sl(bf), sl(st)), r=[stok], w=[btok])
        else:
            P.op("dve", lambda e: e.tensor_copy(sl(bf), sl(st)), r=[stok], w=[btok])
        return bf, btok


def phase_p1(nc, P, D, layer, first, mid_hook=None):
    with contextlib.ExitStack() as es:
        xT = es.enter_context(nc.sbuf_tensor("p1_xT", [128, 8, TOK], BF16))
        win = D["w_in"][layer].rearrange("(kc p) n -> p kc n", p=128)
        es0 = es
        ws = WStream(nc, es0, "p1_w", [128, 8, 512], nstage=2, nbf=2)
        ps = Ring(nc, es0, "p1_ps", [128, 512], F32, 6, psum=True)
        es = es0.enter_context(contextlib.ExitStack())
        if first:
            xst = Ring(nc, es, "p1_xst", [128, 1024], F32, 2)
            for kc in range(8):
                for hf in range(2):
                    st, stok = xst.next()
                    P.op("sp", lambda e, st=st, kc=kc, hf=hf: e.dma_start(
                        out=st[:], in_=D["xT_f32"][kc * 128:(kc + 1) * 128, hf * 1024:(hf + 1) * 1024]),
                        w=[stok], dma=stok)
                    P.op("dve" if hf else "pool", lambda e, st=st, kc=kc, hf=hf: e.tensor_copy(
                        xT[:, kc, hf * 1024:(hf + 1) * 1024], st[:]),
                        r=[stok], w=["xT%d_%d" % (kc, hf)])
                P.op("pool", lambda e, kc=kc: e.dma_start(out=D["xT_a"][kc * 128:(kc + 1) * 128, :], in_=xT[:, kc, :]),
                     r=["xT%d_0" % kc, "xT%d_1" % kc], w=["xT_a_d%d" % kc], dma="st_xT")
            xtoks = ["xT%d_%d" % (kc, hf) for kc in range(8) for hf in range(2)]
        else:
            for kc in range(8):
                P.op("sp", lambda e, kc=kc: e.dma_start(out=xT[:, kc, :], in_=D["xT_a"][kc * 128:(kc + 1) * 128, :]),
                     w=["xT%d" % kc], dma="ld_xT%d" % kc)
            xtoks = ["xT%d" % kc for kc in range(8)]

        ost = Ring(nc, es, "p1_ost", [128, TOK], BF16, 2)
        evk = [0]

        def evac(dst_ap, src_ap, r, w):
            eng = "act" if evk[0] % 2 == 0 else "dve"
            evk[0] += 1
            if eng == "act":
                P.op("act", lambda e: e.copy(dst_ap, src_ap), r=r, w=w)
            else:
                P.op("dve", lambda e: e.tensor_copy(dst_ap, src_ap), r=r, w=w)

        fm_groups = [(0, 512, R_QA), (512, 512, R_KA), (1536, 512, R_QD), (2048, 256, R_QD + 512),
                     (2304, 512, R_KD), (2816, 256, R_KD + 512)]
        for (c0, n, row0) in fm_groups:
            wb, wtok = ws.load(P, win[:, :, c0:c0 + n], sub=(lambda t, n=n: t[:, :, 0:n]))
            for cc in range(n // 128):
                os_, otok = ost.next()
                for tt in range(4):
                    bank, btok = ps.next()
                    mm_group(P, bank[:, :],
                             [(wb[:, kc, cc * 128:(cc + 1) * 128], xT[:, kc, tt * 512:(tt + 1) * 512]) for kc in range(8)],
                             r=[wtok] + xtoks, w=[btok])
                    evac(os_[:, tt * 512:(tt + 1) * 512], bank[:, :], r=[btok], w=[otok + "q%d" % tt])
                r0 = row0 + cc * 128
                P.op("pool", lambda e, os_=os_, r0=r0: e.dma_start(out=D["x1in"][r0:r0 + 128, :], in_=os_[:]),
                     r=[otok + "q%d" % t for t in range(4)], w=["x1in_r%d" % r0], dma=otok)

        wc, wctok = ws.load(P, win[:, :, 4352:4864])
        wh, whtok = ws.load(P, win[:, :, 4864:5376])
        ust = Ring(nc, es, "p1_ust", [128, TOK], F32, 2)
        ctmp = Ring(nc, es, "p1_ctmp", [128, 512], F32, 2)
        uh = es.enter_context(nc.sbuf_tensor("p1_uh", [128, 8], F32))
        for cc in range(4):
            us, utok = ust.next()
            for tt in range(4):
                ba, batok = ps.next()
                bb, bbtok = ps.next()
                mm_group(P, ba[:, :], [(wc[:, kc, cc * 128:(cc + 1) * 128], xT[:, kc, tt * 512:(tt + 1) * 512]) for kc in range(8)],
                         r=[wctok] + xtoks, w=[batok])
                mm_group(P, bb[:, :], [(wh[:, kc, cc * 128:(cc + 1) * 128], xT[:, kc, tt * 512:(tt + 1) * 512]) for kc in range(8)],
                         r=[whtok] + xtoks, w=[bbtok])
                ct, cttok = ctmp.next()
                P.op("act", lambda e, ct=ct, ba=ba: e.copy(ct[:], ba[:, :]), r=[batok], w=[cttok])
                P.op("dve", lambda e, us=us, ct=ct, bb=bb, tt=tt: e.tensor_tensor(us[:, tt * 512:(tt + 1) * 512], ct[:], bb[:, :], ALU.mult),
                     r=[cttok, bbtok], w=[utok + "q%d" % tt])
            P.op("pool", lambda e, us=us, cc=cc: e.dma_start(out=D["u_scr"][cc * 128:(cc + 1) * 128, :], in_=us[:]),
                 r=[utok + "q%d" % t for t in range(4)], w=["u_scr%d" % cc], dma=utok)
            P.op("dve", lambda e, us=us, cc=cc: e.tensor_copy(uh[:, cc * 2:cc * 2 + 2], us[:, TOK - 2:TOK]),
                 r=[utok + "q3"], w=["uh"])
        uz = es.enter_context(nc.sbuf_tensor("p1_uz", [128, 8], F32))
        P.op("pool", lambda e: e.memset(uz[:], 0.0), w=["uz"])
        P.op("sp", lambda e: e.dma_start(out=D["uh_in"][0:128, :], in_=uh[:]), r=["uh"], w=["uh_in"], dma="st_uh")
        P.op("sp", lambda e: e.dma_start(out=D["uh_in"][128:256, :], in_=uz[:]), r=["uz"], w=["uh_inz"], dma="st_uh")
        P.emit()
        es.close()
        es = es0.enter_context(contextlib.ExitStack())
        xtoks = []
        if mid_hook is not None:
            mid_hook()

        wv = es.enter_context(nc.sbuf_tensor("p1_wv", [128, 8, 1280], BF16))
        for (c0, n, d0) in [(1024, 512, 0), (3072, 512, 512), (3584, 256, 1024)]:
            wb, wtok = ws.load(P, win[:, :, c0:c0 + n], sub=(lambda t, n=n: t[:, :, 0:n]))
            P.op("pool", lambda e, wb=wb, n=n, d0=d0: e.tensor_copy(wv[:, :, d0:d0 + n], wb[:, :, 0:n]),
                 r=[wtok], w=["wv%d" % d0])
        vm = es.enter_context(nc.sbuf_tensor("p1_vm", [128, 4, 16, 128], BF16))
        vd = es.enter_context(nc.sbuf_tensor("p1_vd", [128, 4, 16, 192], BF16))
        for t16 in range(16):
            lhs = [xT[:, kc, t16 * 128:(t16 + 1) * 128] for kc in range(8)]
            b0, t0 = ps.next()
            mm_group(P, b0[:, :], [(lhs[kc], wv[:, kc, 0:512]) for kc in range(8)], r=["wv0"] + xtoks, w=[t0])
            evac(vm[:, :, t16, :], b0[:, :].rearrange("p (j c) -> p j c", j=4), r=[t0], w=["vm%d" % t16])
        for g in range(3):
            rr = (1, 4, 16)[g]
            nbl = 16 // rr
            for u in range(16):
                rho, jbl = u // nbl, u % nbl
                st = jbl * 128 * rr + rho
                bq, tq = ps.next()
                mm_group(P, bq[:, 0:256],
                         [(xT[:, kc, st:st + 128 * rr - (rr - 1):rr], wv[:, kc, 512 + g * 256:512 + (g + 1) * 256]) for kc in range(8)],
                         r=["wv512", "wv1024"] + xtoks, w=[tq])
                evac(vd[:, :, u, g * 64:(g + 1) * 64], bq[:, 0:256].rearrange("p (j d) -> p j d", j=4), r=[tq], w=["vd%d_%d" % (g, u)])
        for j in range(4):
            P.op("sp", lambda e, j=j: e.dma_start(out=D["x1in"][R_VM + j * 128:R_VM + (j + 1) * 128, :],
                                                 in_=vm[:, j].rearrange("p a b -> p (a b)")),
                 r=["vm%d" % t for t in range(16)], w=["x1in_vm%d" % j], dma="st_v")
            P.op("pool", lambda e, j=j: e.dma_start(
                out=bass.AP(D["x1in"].tensor, (R_VD + j * 256) * TOK, [[4096, 128], [1, 3072]]),
                in_=vd[:, j].rearrange("p a b -> p (a b)")),
                r=["vd%d_%d" % (g, u) for g in range(3) for u in range(16)], w=["x1in_vd%d" % j], dma="st_v2")
        P.emit()


def dram_specs():
    S = {}
    S["w_in"] = ([DEPTH, 1024, IN_COLS], F32)
    S["xT_f32"] = ([1024, TOK], F32)
    S["x_tm"] = ([TOK, 1024], F32)
    S["xT_a"] = ([1024, TOK], BF16)
    S["xT_b"] = ([1024, TOK], BF16)
    S["x1in"] = ([R_X1, TOK], BF16)
    S["x1g"] = ([4 * R_X1, TOK], BF16)
    S["u_scr"] = ([512, TOK], F32)
    S["uh_in"] = ([256, 8], F32)
    S["uh_g"] = ([4 * 256, 8], F32)
    S["idx"] = ([128, NIDX], I32)
    S["blkind"] = ([32, SEQ], BF16)
    S["gtab"] = ([128, 3, 32, 32], F32)
    S["dmask"] = ([128, 4, 512], BF16)
    S["ident"] = ([128, 128], BF16)
    S["dilmask"] = ([128, 2, 128], BF16)
    S["x2in"] = ([192, SEQ], BF16)
    S["x2g"] = ([4 * 192, SEQ], BF16)
    S["w_moba_proj"] = ([DEPTH, 512, 1024], F32)
    S["w_dil_proj"] = ([DEPTH, 256, 1024], F32)
    S["w_conv_proj"] = ([DEPTH, 512, 1024], F32)
    S["w_mix_out"] = ([DEPTH, 1024, 1024], F32)
    S["w_up"] = ([DEPTH, 1024, 2 * D_FF], F32)
    S["w_down"] = ([DEPTH, D_FF, 1024], F32)
    for nm in ("ln1_g", "ln1_b", "ln2_g", "ln2_b"):
        S[nm] = ([DEPTH, 1024], F32)
    S["wsc"] = ([DEPTH, 128, 4, 3], F32)
    S["wfc"] = ([DEPTH, 128, 44, 3], F32)
    S["bfc"] = ([DEPTH, 128, 44], F32)
    S["xres_a"] = ([TOK, 1024], F32)
    S["xres_b"] = ([TOK, 1024], F32)
    S["out"] = ([TOK, 1024], F32)
    S["x3in"] = ([256, 16], BF16)
    S["x3g"] = ([4 * 256, 16], BF16)
    S["wup_bf"] = ([22, 128, 2048], BF16)
    S["dbg_m"] = ([1024, TOK], BF16)
    S["dbg_yc"] = ([512, TOK], BF16)
    return S


def make_D(nc, names_in, names_out, names_int=(), layers=DEPTH):
    S = dram_specs()
    D = {}
    for nm in names_in:
        shp, dt = S[nm]
        shp = list(shp)
        if shp[0] == DEPTH and len(shp) >= 2 and nm not in ("x3in",):
            shp[0] = layers
        D[nm] = nc.dram_tensor(nm, shp, dt, kind="ExternalInput").ap()
    for nm in names_out:
        shp, dt = S[nm]
        D[nm] = nc.dram_tensor(nm, list(shp), dt, kind="ExternalOutput").ap()
    for nm in names_int:
        shp, dt = S[nm]
        D[nm] = nc.dram_tensor(nm, list(shp), dt).ap()
    return D


def host_consts():
    import ml_dtypes
    C = {}
    t = np.arange(SEQ)
    C["blkind"] = (t[None, :] // 256 == np.arange(32)[:, None]).astype(np.float32).astype(ml_dtypes.bfloat16)
    n = np.arange(32)
    qb = np.arange(32)[:, None]
    past = (n[None, :] < qb).astype(np.float32)
    own = (n[None, :] == qb).astype(np.float32)
    pastbias = np.where(past > 0, 0.0, -1e30).astype(np.float32)
    past30k = (30000.0 * past).astype(np.float32)
    t2 = np.where(past > 0, -30000.0, np.where(own > 0, 0.0, -30000.0)).astype(np.float32)
    gt = np.stack([pastbias, past30k, t2], 0)
    C["gtab"] = np.ascontiguousarray(np.broadcast_to(gt[None], (128, 3, 32, 32))).astype(np.float32)
    i = np.arange(128)[:, None]
    m = np.ones((128, 4, 512), np.float32)
    for a in range(4):
        for b in range(4):
            jq = np.arange(128)[None, :]
            if a // 2 == b // 2:
                if a % 2 > b % 2:
                    blk = np.zeros((128, 128), np.float32)
                elif a % 2 == b % 2:
                    blk = (i <= jq).astype(np.float32)
                else:
                    blk = np.ones((128, 128), np.float32)
                m[:, a, b * 128:(b + 1) * 128] = blk
    C["dmask"] = m.astype(ml_dtypes.bfloat16)
    C["ident"] = np.eye(128, dtype=np.float32).astype(ml_dtypes.bfloat16)
    jq = np.arange(128)[None, :]
    dm = np.stack([(i >= jq), (i <= jq)], 1).astype(np.float32)
    C["dilmask"] = dm.astype(ml_dtypes.bfloat16)
    return C


def attn_finalize(P, acc, acctok, osb_ring, bc_ring, ones_f, ydst_ap, wtoks, nq=512):
    osb, otok = osb_ring.next()
    P.op("act", lambda e: e.copy(osb[0:65, 0:nq], acc[0:65, 0:nq]), r=[acctok], w=[otok])
    P.op("dve", lambda e: e.reciprocal(osb[64:65, 0:nq], osb[64:65, 0:nq]), r=[otok], w=[otok + "r"])
    bc, bctok = bc_ring.next()
    P.op("pe", lambda e: e.matmul(bc[0:64, 0:nq], ones_f[64:65, 0:64], osb[64:65, 0:nq], start=True, stop=True),
         r=[otok + "r"], w=[bctok])
    P.op("dve", lambda e: e.tensor_tensor(ydst_ap, osb[0:64, 0:nq], bc[0:64, 0:nq], ALU.mult),
         r=[otok, bctok], w=wtoks)


def phase_p2a(nc, P, D):
    x1g3 = D["x1g"].rearrange("(r n) t -> r n t", r=4)
    with contextlib.ExitStack() as es:
        qaug = [es.enter_context(nc.sbuf_tensor("qaug%d" % h, [96, SEQ], BF16)) for h in range(2)]
        kaug = [es.enter_context(nc.sbuf_tensor("kaug%d" % h, [96, SEQ], BF16)) for h in range(2)]
        vst = es.enter_context(nc.sbuf_tensor("vst", [128, 64, 128], BF16))
        vaug = [es.enter_context(nc.sbuf_tensor("vaug%d" % h, [128, 64, 65], BF16)) for h in range(2)]
        gtab = es.enter_context(nc.sbuf_tensor("sb_gtab", [128, 3, 32, 32], F32))
        dmask = es.enter_context(nc.sbuf_tensor("sb_dmask", [128, 4, 512], BF16))
        ident = es.enter_context(nc.sbuf_tensor("sb_ident", [128, 128], BF16))
        ones_f = es.enter_context(nc.sbuf_tensor("ones_f", [128, 64], F32))
        ystage = [es.enter_context(nc.sbuf_tensor("ystage%d" % h, [64, SEQ], BF16)) for h in range(2)]
        km = es.enter_context(nc.sbuf_tensor("km", [64, 2, 32], F32))
        kmh = es.enter_context(nc.sbuf_tensor("kmh", [64, 2, 2, 32], BF16))
        gsb = Ring(nc, es, "gsb", [128, 32], F32, 4)
        gm8 = Ring(nc, es, "gm8", [128, 8], F32, 4)
        gmm = Ring(nc, es, "gmm", [128, 32], F32, 4)
        gfb = Ring(nc, es, "gfb", [128, 32], BF16, 4)
        pT = Ring(nc, es, "pT", [128, 512], BF16, 5)
        osb = Ring(nc, es, "osb", [65, 512], F32, 2)
        ps_s = Ring(nc, es, "ps_s", [128, 512], F32, 4, psum=True)
        ps_acc = Ring(nc, es, "ps_acc", [128, 512], F32, 1, psum=True)
        ps_g = Ring(nc, es, "ps_g", [128, 512], F32, 1, psum=True)
        ps_t = Ring(nc, es, "ps_t", [128, 512], F32, 1, psum=True)
        ps_bc = Ring(nc, es, "ps_bc", [128, 512], F32, 1, psum=True)

        P.op("sp", lambda e: e.dma_start(out=gtab[:], in_=D["gtab"][:, :, :, :]), w=["gtab"], dma="c_gtab")
        P.op("sp", lambda e: e.dma_start(out=dmask[:], in_=D["dmask"][:, :, :]), w=["dmask"], dma="c_dmask")
        P.op("sp", lambda e: e.dma_start(out=ident[:], in_=D["ident"][:, :]), w=["ident"], dma="c_ident")
        P.op("pool", lambda e: e.memset(ones_f[:], 1.0), w=["ones_f"])
        sidx = es.enter_context(nc.sbuf_tensor("sb_idx", [128, NIDX], I32))
        P.op("sp", lambda e: e.dma_start(out=sidx[:], in_=D["idx"][:, :]), w=["idx"], dma="c_idx")

        def gather(dst, table, col, npart, wtok, chan, extra=()):
            P.op("pool", lambda e: e.indirect_dma_start(
                out=dst, out_offset=None, in_=table,
                in_offset=bass.IndirectOffsetOnAxis(ap=sidx[0:npart, col:col + 1], axis=0)),
                r=["idx"] + list(extra), w=[wtok], dma=chan)

        for h in range(2):
            P.op("sp", lambda e, h=h: e.dma_start(out=kaug[h][64:96, :], in_=D["blkind"][:, :]), w=["kind%d" % h], dma="c_ind%d" % h)
            for r in range(4):
                gather(kaug[h][0:64, r * TOK:(r + 1) * TOK], D["x1g"][:, :], IDX_K + h * 4 + r, 64, "k%d_%d" % (h, r), "ld_k%d_%d" % (h, r), ["x1g_c2", "x1g_c3"])
                gather(qaug[h][0:64, r * TOK:(r + 1) * TOK], D["x1g"][:, :], IDX_Q + h * 4 + r, 64, "q%d_%d" % (h, r), "ld_q%d_%d" % (h, r), ["x1g_c0", "x1g_c1"])
        for r in range(4):
            gather(vst[:, r * 16:(r + 1) * 16, :].rearrange("p a b -> p (a b)"), D["x1g"][:, :], IDX_V + r, 128, "vst%d" % r, "ld_v%d" % r, ["x1g_c10", "x1g_c11"])
        for h in range(2):
            P.op("pool", lambda e, h=h: e.memset(vaug[h][:, :, 64:65], 1.0), w=["vone%d" % h])
            for r in range(4):
                P.op("pool", lambda e, h=h, r=r: e.tensor_copy(vaug[h][:, r * 16:(r + 1) * 16, 0:64],
                                                             vst[:, r * 16:(r + 1) * 16, h * 64:(h + 1) * 64]),
                     r=["vst%d" % r], w=["v%d_%d" % (h, r)])
        for h in range(2):
            for r in range(4):
                P.op("dve", lambda e, h=h, r=r: e.tensor_reduce(
                    km[:, h, r * 8:(r + 1) * 8], kaug[h][0:64, r * TOK:(r + 1) * TOK].rearrange("p (n t) -> p n t", t=256),
                    AX.X, ALU.add), r=["k%d_%d" % (h, r)], w=["km%d_%d" % (h, r)])
            kr = ["km%d_%d" % (h, r) for r in range(4)]
            P.op("dve", lambda e, h=h: e.tensor_scalar(km[:, h, :], km[:, h, :], 1.0 / 256.0, None, op0=ALU.mult), r=kr, w=["kms%d" % h])
            P.op("dve", lambda e, h=h: e.tensor_copy(kmh[:, h, 0, :], km[:, h, :]), r=["kms%d" % h], w=["kmhi%d" % h])
            P.op("dve", lambda e, h=h: e.tensor_tensor(kmh[:, h, 1, :], km[:, h, :], kmh[:, h, 0, :], ALU.subtract),
                 r=["kms%d" % h, "kmhi%d" % h], w=["kmlo%d" % h])

        def gating(h, T):
            tb, ttok = ps_t.next()
            ftoks = []
            for c4 in range(4):
                qc = 4 * T + c4
                qb = qc // 2
                rk = qc // 16
                gp, gptok = ps_g.next()
                qsl = qaug[h][0:64, qc * 128:(qc + 1) * 128]
                P.op("pe", lambda e, gp=gp, qsl=qsl: e.matmul(gp[:, 0:32], qsl, kmh[:, h, 0, :], start=True, stop=False),
                     r=["q%d_%d" % (h, rk), "kmhi%d" % h], w=[gptok])
                P.op("pe", lambda e, gp=gp, qsl=qsl: e.matmul(gp[:, 0:32], qsl, kmh[:, h, 1, :], start=False, stop=True),
                     r=["q%d_%d" % (h, rk), "kmlo%d" % h], w=[gptok])
                g, gtok = gsb.next()
                m8, m8tok = gm8.next()
                mm, mmtok = gmm.next()
                fb, fbtok = gfb.next()
                P.op("dve", lambda e, g=g, gp=gp, qb=qb: e.tensor_tensor(g[:], gp[:, 0:32], gtab[:, 0, qb, :], ALU.add),
                     r=[gptok, "gtab"], w=[gtok])
                P.op("dve", lambda e, g=g, m8=m8: e.max(m8[:], g[:]), r=[gtok], w=[m8tok])
                P.op("dve", lambda e, g=g, m8=m8, mm=mm: e.tensor_scalar(mm[:], g[:], m8[:, 2:3], None, op0=ALU.is_ge),
                     r=[gtok, m8tok], w=[mmtok])
                P.op("dve", lambda e, mm=mm, qb=qb: e.tensor_tensor(mm[:], mm[:], gtab[:, 1, qb, :], ALU.mult),
                     r=[mmtok], w=[mmtok + "b"])
                P.op("dve", lambda e, mm=mm, fb=fb, qb=qb: e.tensor_tensor(fb[:], mm[:], gtab[:, 2, qb, :], ALU.add),
                     r=[mmtok + "b"], w=[fbtok])
                P.op("pe", lambda e, tb=tb, fb=fb, c4=c4: e.matmul(tb[0:32, c4 * 128:(c4 + 1) * 128], fb[:], ident[:], start=True, stop=True),
                     r=[fbtok, "ident"], w=[ttok + "c%d" % c4])
            P.op("act", lambda e, tb=tb: e.copy(qaug[h][64:96, T * 512:(T + 1) * 512], tb[0:32, :]),
                 r=[ttok + "c%d" % c for c in range(4)], w=["qbias%d_%d" % (h, T)])

        def attend(h, T):
            acc, acctok = ps_acc.next()
            nkc = 4 * T + 4
            LOOK = 3
            sbanks = {}

            def issue_qk(kc):
                sb_, stok = ps_s.next()
                P.op("pe", lambda e, sb_=sb_, kc=kc: e.matmul(sb_[:, :], kaug[h][0:96, kc * 128:(kc + 1) * 128],
                                                              qaug[h][0:96, T * 512:(T + 1) * 512], start=True, stop=True),
                     r=["k%d_%d" % (h, kc // 16), "kind%d" % h, "q%d_%d" % (h, T // 4), "qbias%d_%d" % (h, T)], w=[stok])
                sbanks[kc] = (sb_, stok)

            for kc in range(min(LOOK, nkc)):
                issue_qk(kc)
            for kc in range(nkc):
                sb_, stok = sbanks.pop(kc)
                pt, pttok = pT.next()
                P.op("act", lambda e, pt=pt, sb_=sb_: e.activation(out=pt[:], in_=sb_[:, :], func=AF.Exp, scale=0.125),
                     r=[stok], w=[pttok])
                if kc >= 4 * T:
                    a = kc - 4 * T
                    P.op("dve", lambda e, pt=pt, a=a: e.tensor_tensor(pt[:], pt[:], dmask[:, a, :], ALU.mult),
                         r=[pttok, "dmask"], w=[pttok])
                if kc + LOOK < nkc:
                    issue_qk(kc + LOOK)
                P.op("pe", lambda e, acc=acc, pt=pt, kc=kc: e.matmul(acc[0:65, :], vaug[h][:, kc, 0:65], pt[:],
                                                                     start=(kc == 0), stop=(kc == nkc - 1)),
                     r=[pttok, "v%d_%d" % (h, kc // 16), "vone%d" % h], w=[acctok])
            attn_finalize(P, acc, acctok, osb, ps_bc, ones_f, ystage[h][:, T * 512:(T + 1) * 512],
                          ["y%d_%d" % (h, T)])

        gating(0, 0)
        for h in range(2):
            for T in range(16):
                if T + 1 < 16:
                    gating(h, T + 1)
                elif h == 0:
                    gating(1, 0)
                attend(h, T)
            P.op("pool", lambda e, h=h: e.dma_start(out=D["x2in"][h * 64:(h + 1) * 64, :], in_=ystage[h][:]),
                 r=["y%d_%d" % (h, T) for T in range(16)], w=["x2in%d" % h], dma="st_y%d" % h)
        P.emit()


def phase_p2b(nc, P, D):
    with contextlib.ExitStack() as es:
        sidx = es.enter_context(nc.sbuf_tensor("sb_idx2", [128, NIDX], I32))
        qd = Ring(nc, es, "qd", [64, SEQ], BF16, 2)
        kd = Ring(nc, es, "kd", [64, SEQ], BF16, 2)
        vdst = es.enter_context(nc.sbuf_tensor("vdst", [128, 64, 192], BF16))
        vaug = [es.enter_context(nc.sbuf_tensor("vdaug%d" % g, [128, 64, 65], BF16)) for g in range(3)]
        accs = es.enter_context(nc.sbuf_tensor("accs", [65, SEQ], F32))
        dilmask = es.enter_context(nc.sbuf_tensor("sb_dilmask", [128, 2, 128], BF16))
        ones_f = es.enter_context(nc.sbuf_tensor("ones_f2", [128, 64], F32))
        ystage = es.enter_context(nc.sbuf_tensor("ystage_d", [64, SEQ], BF16))
        rrow = Ring(nc, es, "rrow", [65, 512], F32, 2)
        pT = Ring(nc, es, "pTd", [128, 256], BF16, 4)
        ps_s = Ring(nc, es, "psd_s", [128, 512], F32, 3, psum=True)
        ps_acc = Ring(nc, es, "psd_acc", [128, 512], F32, 3, psum=True)
        ps_bc = Ring(nc, es, "psd_bc", [128, 512], F32, 2, psum=True)

        P.op("sp", lambda e: e.dma_start(out=sidx[:], in_=D["idx"][:, :]), w=["idx"], dma="c_idx")
        P.op("sp", lambda e: e.dma_start(out=dilmask[:], in_=D["dilmask"][:, :, :]), w=["dilmask"], dma="c_dilmask")
        P.op("pool", lambda e: e.memset(ones_f[:], 1.0), w=["ones_f"])

        def gather(dst, table, col, npart, wtok, chan):
            P.op("pool", lambda e: e.indirect_dma_start(
                out=dst, out_offset=None, in_=table,
                in_offset=bass.IndirectOffsetOnAxis(ap=sidx[0:npart, col:col + 1], axis=0)),
                r=["idx"], w=[wtok], dma=chan)

        vflat = vdst[:].rearrange("p a b -> p (a b)")
        for r in range(4):
            gather(vflat[:, r * 3072:r * 3072 + 2048], D["x1g"][:, :], IDX_DV + r * 2, 128, "vdst%d_0" % r, "ld_dv%d_0" % r)
            gather(vflat[:, r * 3072 + 2048:(r + 1) * 3072], D["x1g"].rearrange("n (h c) -> (n h) c", h=2), IDX_DV + r * 2 + 1, 128, "vdst%d_1" % r, "ld_dv%d_1" % r)
        for g in range(3):
            P.op("dve", lambda e, g=g: e.memset(vaug[g][:, :, 64:65], 1.0), w=["vone%d" % g])
            for r in range(4):
                P.op("dve", lambda e, g=g, r=r: e.tensor_copy(vaug[g][:, r * 16:(r + 1) * 16, 0:64],
                                                             vdst[:, r * 16:(r + 1) * 16, g * 64:(g + 1) * 64]),
                     r=["vdst%d_0" % r, "vdst%d_1" % r], w=["v%d_%d" % (g, r)])

        def finalize_tile(T):
            sl = slice(T * 512, (T + 1) * 512)
            rw, rtok = rrow.next()
            P.op("dve", lambda e, rw=rw, sl=sl: e.reciprocal(rw[64:65, :], accs[64:65, sl]), r=["accs%d" % T], w=[rtok])
            bc, bctok = ps_bc.next()
            P.op("pe", lambda e, bc=bc, rw=rw: e.matmul(bc[0:64, :], ones_f[64:65, 0:64], rw[64:65, :], start=True, stop=True),
                 r=[rtok, "ones_f"], w=[bctok])
            P.op("dve", lambda e, bc=bc, sl=sl: e.tensor_tensor(ystage[:, sl], accs[0:64, sl], bc[0:64, :], ALU.mult),
                 r=["accs%d" % T, bctok], w=["yd%d" % T])

        import os
        GSEL = [int(c) for c in os.environ.get("DIL_GROUPS", "012")]
        for g in GSEL:
            rr = (1, 4, 16)[g]
            nbl = 16 // rr
            nbk = 64 // rr
            q, qtok = qd.next()
            k, ktok = kd.next()
            for r in range(4):
                gather(q[0:64, r * TOK:(r + 1) * TOK], D["x1g"][:, :], IDX_DQ + g * 4 + r, 64, qtok + "_%d" % r, "ld_dq%d_%d" % (g, r))
                gather(k[0:64, r * TOK:(r + 1) * TOK], D["x1g"][:, :], IDX_DK + g * 4 + r, 64, ktok + "_%d" % r, "ld_dk%d_%d" % (g, r))
            qk_r = [qtok + "_%d" % r for r in range(4)] + [ktok + "_%d" % r for r in range(4)]

            def tok_slice(rho, jb):
                st = rho + rr * 128 * jb
                return slice(st, st + 128 * rr - (rr - 1), rr)

            def vunit(rho, jb):
                R, jbl = jb // nbl, jb % nbl
                return R * 16 + rho * nbl + jbl, R

            if rr == 1:
                batches = [[(0, 4 * a + i) for i in range(4)] for a in range(16)]
            elif rr == 4:
                batches = [[(rho, jb) for rho in range(4)] for jb in range(16)]
            else:
                batches = [[(4 * a + i, jb) for i in range(4)] for jb in range(4) for a in range(4)]
            flat_units = [(bi, ui, rho, jb) for bi, batch in enumerate(batches) for ui, (rho, jb) in enumerate(batch)]
            qk_done = {}

            def issue_qk(n):
                bi, ui, rho, jb = flat_units[n]
                sb_, stok = ps_s.next()
                qs = q[0:64, tok_slice(rho, jb)]
                if jb > 0:
                    kprev = k[0:64, tok_slice(rho, jb - 1)]
                    P.op("pe", lambda e, sb_=sb_, qs=qs, kprev=kprev: e.matmul(
                        sb_[:, 0:128], kprev, qs, start=True, stop=True), r=qk_r, w=[stok])
                kown = k[0:64, tok_slice(rho, jb)]
                P.op("pe", lambda e, sb_=sb_, qs=qs, kown=kown: e.matmul(
                    sb_[:, 128:256], kown, qs, start=True, stop=True), r=qk_r, w=[stok])
                qk_done[n] = (sb_, stok)

            LOOK = 2
            for n in range(min(LOOK, len(flat_units))):
                issue_qk(n)
            n_unit = 0
            for bi, batch in enumerate(batches):
                acc, acctok = ps_acc.next()
                for ui, (rho, jb) in enumerate(batch):
                    sb_, stok = qk_done.pop(n_unit)
                    c0 = 0 if jb > 0 else 128
                    pt, pttok = pT.next()
                    P.op("act", lambda e, pt=pt, sb_=sb_, c0=c0: e.activation(out=pt[:, c0:256], in_=sb_[:, c0:256], func=AF.Exp, scale=0.125),
                         r=[stok], w=[pttok])
                    P.op("dve", lambda e, pt=pt, c0=c0: e.tensor_tensor(
                        pt[:, c0:256], pt[:, c0:256], dilmask[:].rearrange("p a b -> p (a b)")[:, c0:256], ALU.mult),
                        r=[pttok, "dilmask"], w=[pttok])
                    if n_unit + LOOK < len(flat_units):
                        issue_qk(n_unit + LOOK)
                    n_unit += 1
                    osl = acc[0:65, ui * 128:(ui + 1) * 128]
                    if jb > 0:
                        vu, R = vunit(rho, jb - 1)
                        vprev = vaug[g][:, vu, 0:65]
                        P.op("pe", lambda e, osl=osl, pt=pt, vprev=vprev: e.matmul(osl, vprev, pt[:, 0:128], start=True, stop=False),
                             r=[pttok, "v%d_%d" % (g, R), "vone%d" % g], w=[acctok])
                    vu, R = vunit(rho, jb)
                    vown = vaug[g][:, vu, 0:65]
                    P.op("pe", lambda e, osl=osl, pt=pt, vown=vown, jb=jb: e.matmul(osl, vown, pt[:, 128:256], start=(jb == 0), stop=True),
                         r=[pttok, "v%d_%d" % (g, R), "vone%d" % g], w=[acctok])
                if rr == 1:
                    dst = accs[0:65, bi * 512:(bi + 1) * 512]
                    src = acc[0:65, 0:512]
                elif rr == 4:
                    dst = accs[0:65, bi * 512:(bi + 1) * 512].rearrange("p (i rho) -> p rho i", rho=4)
                    src = acc[0:65, 0:512].rearrange("p (rho i) -> p rho i", rho=4)
                else:
                    jb, a = bi // 4, bi % 4
                    dst = accs[0:65, jb * 2048:(jb + 1) * 2048].rearrange("p (i rho) -> p rho i", rho=16)[:, 4 * a:4 * a + 4, :]
                    src = acc[0:65, 0:512].rearrange("p (rho i) -> p rho i", rho=4)
                if rr == 1:
                    tiles = [bi]
                elif rr == 4:
                    tiles = [bi]
                else:
                    tiles = [4 * (bi // 4) + t for t in range(4)]
                atoks = ["accs%d" % t for t in tiles]
                if g == GSEL[0]:
                    P.op("act", lambda e, dst=dst, src=src: e.copy(dst, src), r=[acctok], w=atoks)
                else:
                    P.op("dve", lambda e, dst=dst, src=src: e.tensor_tensor(dst, dst, src, ALU.add), r=[acctok] + atoks, w=atoks)
                if g == GSEL[-1] and rr == 16 and bi % 4 == 3:
                    for t in tiles:
                        finalize_tile(t)
        P.op("sp", lambda e: e.dma_start(out=D["x2in"][128:192, :], in_=ystage[:]),
             r=["yd%d" % T for T in range(16)], w=["x2in_d"], dma="st_yd")
        P.emit()


class LNCtx:
    def __init__(self, nc, es, pfx, g_ap, b_ap, ident):
        self.nc = nc
        self.pfx = pfx
        self.xt = Ring(nc, es, pfx + "_xt", [128, 1024], F32, 2)
        self.rt = Ring(nc, es, pfx + "_rt", [128, 1024], F32, 2)
        self.yo = Ring(nc, es, pfx + "_yo", [128, 1024], F32, 2)
        self.xb = Ring(nc, es, pfx + "_xb", [128, 1024], BF16, 2)
        self.st = Ring(nc, es, pfx + "_st", [128, 2, 6], F32, 2)
        self.mv = Ring(nc, es, pfx + "_mv", [128, 4], F32, 2)
        self.gbc = es.enter_context(nc.sbuf_tensor(pfx + "_gbc", [128, 1024], F32))
        self.bbc = es.enter_context(nc.sbuf_tensor(pfx + "_bbc", [128, 1024], F32))
        self.g_ap, self.b_ap, self.ident = g_ap, b_ap, ident
        self.epsb = es.enter_context(nc.sbuf_tensor(pfx + "_eps", [128, 1], F32))

    def load_consts(self, P):
        gsrc = bass.AP(self.g_ap.tensor, self.g_ap.offset, [[0, 128], [1, 1024]])
        bsrc = bass.AP(self.b_ap.tensor, self.b_ap.offset, [[0, 128], [1, 1024]])
        P.op("sp", lambda e: e.dma_start(out=self.gbc[:], in_=gsrc), w=[self.pfx + "gbc"], dma=self.pfx + "c_g")
        P.op("sp", lambda e: e.dma_start(out=self.bbc[:], in_=bsrc), w=[self.pfx + "bbc"], dma=self.pfx + "c_b")
        P.op("dve", lambda e: e.memset(self.epsb[:], LN_EPS), w=[self.pfx + "eps"])

    def tile(self, P, banks, btoks, x_src_ap, y_dst_ap, ydtok, xT_dst, xTtok, ps_ring):
        pfx = self.pfx
        xt, xttok = self.xt.next()
        P.op("sp", lambda e: e.dma_start(out=xt[:], in_=x_src_ap), w=[xttok], dma=xttok)
        rt, rttok = self.rt.next()
        for hf in range(2):
            P.op("dve", lambda e, hf=hf: e.scalar_tensor_tensor(
                out=rt[:, hf * 512:(hf + 1) * 512], in0=xt[:, hf * 512:(hf + 1) * 512], scalar=ALPHA,
                in1=banks[hf][:, :], op0=ALU.mult, op1=ALU.add), r=[xttok, btoks[hf]], w=[rttok + "h%d" % hf])
        st, sttok = self.st.next()
        mv, mvtok = self.mv.next()
        for hf in range(2):
            P.op("dve", lambda e, hf=hf: e.bn_stats(st[:, hf, :], rt[:, hf * 512:(hf + 1) * 512]),
                 r=[rttok + "h%d" % hf], w=[sttok + "h%d" % hf])
        P.op("dve", lambda e: e.bn_aggr(mv[:, 0:2], st[:].rearrange("p a b -> p (a b)")),
             r=[sttok + "h0", sttok + "h1"], w=[mvtok])
        P.op("act", lambda e: e.activation(out=mv[:, 2:3], in_=mv[:, 1:2], func=AF.Sqrt, bias=self.epsb[:, 0:1], scale=1.0),
             r=[mvtok, pfx + "eps"], w=[mvtok + "s"])
        P.op("dve", lambda e: e.reciprocal(mv[:, 2:3], mv[:, 2:3]), r=[mvtok + "s"], w=[mvtok + "r"])
        P.op("dve", lambda e: e.tensor_scalar(rt[:], rt[:], mv[:, 0:1], mv[:, 2:3], op0=ALU.subtract, op1=ALU.mult),
             r=[rttok + "h0", rttok + "h1", mvtok + "r"], w=[rttok + "n"])
        yo, yotok = self.yo.next()
        P.op("dve", lambda e: e.tensor_tensor(yo[:], rt[:], self.gbc[:], ALU.mult), r=[rttok + "n", pfx + "gbc"], w=[yotok + "m"])
        P.op("pool", lambda e: e.tensor_tensor(yo[:], yo[:], self.bbc[:], ALU.add), r=[yotok + "m", pfx + "bbc"], w=[yotok])
        P.op("sp", lambda e: e.dma_start(out=y_dst_ap, in_=yo[:]), r=[yotok], w=[ydtok], dma=yotok)
        xb, xbtok = self.xb.next()
        P.op("act", lambda e: e.copy(xb[:], yo[:]), r=[yotok], w=[xbtok])
        for q4 in range(2):
            tb, tbtok = ps_ring.next()
            for k4 in range(4):
                kc = q4 * 4 + k4
                P.op("pe", lambda e, tb=tb, k4=k4, kc=kc: e.matmul(tb[:, k4 * 128:(k4 + 1) * 128], xb[:, kc * 128:(kc + 1) * 128],
                                                                     self.ident[:], start=True, stop=True),
                     r=[xbtok, "ident"], w=[tbtok + "k%d" % k4])
            dst = xT_dst[:, q4 * 4:(q4 + 1) * 4, :]
            src = tb[:, :].rearrange("p (k t) -> p k t", k=4)
            P.op("act", lambda e, dst=dst, src=src: e.copy(dst, src), r=[tbtok + "k%d" % k for k in range(4)],
                 w=[xTtok + ("a" if q4 == 0 else "b")])


def phase_p3a(nc, P, D, layer, x_in_name):
    win = D["w_in"][layer].rearrange("(kc p) n -> p kc n", p=128)
    with contextlib.ExitStack() as es0:
        xT = es0.enter_context(nc.sbuf_tensor("p3_xT", [128, 8, TOK], BF16))
        ycT = es0.enter_context(nc.sbuf_tensor("p3_ycT", [128, 4, TOK], BF16))
        mergedT = es0.enter_context(nc.sbuf_tensor("p3_mT", [128, 8, TOK], BF16))
        sidx = es0.enter_context(nc.sbuf_tensor("sb_idx3", [128, NIDX], I32))
        ident = es0.enter_context(nc.sbuf_tensor("sb_ident3", [128, 128], BF16))
        ps = Ring(nc, es0, "p3_ps", [128, 512], F32, 8, psum=True)

        def gather(dst, table, col, p0, npart, wtok, chan):
            P.op("pool", lambda e: e.indirect_dma_start(
                out=dst, out_offset=None, in_=table,
                in_offset=bass.IndirectOffsetOnAxis(ap=sidx[p0:p0 + npart, col:col + 1], axis=0)),
                r=["idx"], w=[wtok], dma=chan)

        with contextlib.ExitStack() as es:
            u = es.enter_context(nc.sbuf_tensor("p3_u", [128, 4, TOK + 16], F32))
            uh = es.enter_context(nc.sbuf_tensor("p3_uh", [128, 8], F32))
            wsc = es.enter_context(nc.sbuf_tensor("p3_wsc", [128, 4, 3], F32))
            cv = Ring(nc, es, "p3_cv", [128, TOK], F32, 2)
            ws = WStream(nc, es, "p3_wb", [128, 8, 512], nstage=1, nbf=1)
            P.op("sp", lambda e: e.dma_start(out=sidx[:], in_=D["idx"][:, :]), w=["idx"], dma="c_idx")
            P.op("sp", lambda e: e.dma_start(out=ident[:], in_=D["ident"][:, :]), w=["ident"], dma="c_ident")
            P.op("sp", lambda e: e.dma_start(out=wsc[:], in_=D["wsc"][layer]), w=["wsc"], dma="c_wsc")
            for kc in range(8):
                P.op("sp", lambda e, kc=kc: e.dma_start(out=xT[:, kc, :], in_=D["xT_a"][kc * 128:(kc + 1) * 128, :]),
                     w=["xT%d" % kc], dma="ld_xT%d" % kc)
            xtoks = ["xT%d" % kc for kc in range(8)]
            gather(uh[:], D["uh_g"][:, :], IDX_HALO, 0, 128, "uh", "ld_uh")
            for cc in range(4):
                P.op("sp", lambda e, cc=cc: e.dma_start(out=u[:, cc, 16:TOK + 16], in_=D["u_scr"][cc * 128:(cc + 1) * 128, :]),
                     w=["u%d" % cc], dma="ld_u%d" % cc)
                P.op("dve", lambda e, cc=cc: e.tensor_copy(u[:, cc, 14:16], uh[:, cc * 2:cc * 2 + 2]), r=["uh"], w=["uhalo%d" % cc])
            wb, wbtok = ws.load(P, win[:, :, 3840:4352])
            for cc in range(4):
                c, ctok = cv.next()
                ur = ["u%d" % cc, "uhalo%d" % cc, "wsc"]
                P.op("dve", lambda e, c=c, cc=cc: e.tensor_scalar(c[:], u[:, cc, 16:TOK + 16], wsc[:, cc, 0:1], None, op0=ALU.mult),
                     r=ur, w=[ctok + "a"])
                P.op("dve", lambda e, c=c, cc=cc: e.scalar_tensor_tensor(out=c[:], in0=u[:, cc, 15:TOK + 15], scalar=wsc[:, cc, 1:2],
                                                                       in1=c[:], op0=ALU.mult, op1=ALU.add),
                     r=ur + [ctok + "a"], w=[ctok + "b"])
                P.op("dve", lambda e, c=c, cc=cc: e.scalar_tensor_tensor(out=c[:], in0=u[:, cc, 14:TOK + 14], scalar=wsc[:, cc, 2:3],
                                                                       in1=c[:], op0=ALU.mult, op1=ALU.add),
                     r=ur + [ctok + "b"], w=[ctok])
                for tt in range(4):
                    bank, btok = ps.next()
                    mm_group(P, bank[:, :], [(wb[:, kc, cc * 128:(cc + 1) * 128], xT[:, kc, tt * 512:(tt + 1) * 512]) for kc in range(8)],
                             r=[wbtok] + xtoks, w=[btok])
                    P.op("dve", lambda e, c=c, cc=cc, tt=tt, bank=bank: e.tensor_tensor(
                        ycT[:, cc, tt * 512:(tt + 1) * 512], c[:, tt * 512:(tt + 1) * 512], bank[:, :], ALU.mult),
                        r=[ctok, btok], w=["yc%d_%d" % (cc, tt)])
            P.emit()

        import os
        STOP = int(os.environ.get("P3A_STOP", "9"))
        if STOP <= 1:
            return
        with contextlib.ExitStack() as es:
            ymT = es.enter_context(nc.sbuf_tensor("p3_ymT", [128, 4, TOK], BF16))
            ydT = es.enter_context(nc.sbuf_tensor("p3_ydT", [64, 4, TOK], BF16))
            wst = Ring(nc, es, "p3_wst", [128, 36, 128], F32, 2)
            wbf = Ring(nc, es, "p3_wbf", [128, 36, 128], BF16, 2)
            sg = Ring(nc, es, "p3_sg", [128, 512], F32, 6)
            mt = Ring(nc, es, "p3_mt", [128, 512], F32, 6)
            x2v = D["x2g"].rearrange("n (i t) -> (n i) t", i=4)
            for r in range(4):
                gather(ymT[:, r, :], x2v, IDX_YM + r, 0, 128, "ym%d" % r, "ld_ym%d" % r)
                gather(ydT[0:64, r, :], x2v, IDX_YD + r, 0, 64, "yd%d" % r, "ld_yd%d" % r)
            ytoks = ["ym%d" % r for r in range(4)] + ["yd%d" % r for r in range(4)]
            wmp = D["w_moba_proj"][layer].rearrange("(kc p) n -> p kc n", p=128)
            wcp = D["w_conv_proj"][layer].rearrange("(kc p) n -> p kc n", p=128)
            wdp = D["w_dil_proj"][layer].rearrange("(kc p) n -> p kc n", p=64)
            for oc in range(8):
                st, sttok = wst.next()
                bf, bftok = wbf.next()
                cs = slice(oc * 128, (oc + 1) * 128)
                srcs = [(0, 4, 128, wmp[:, :, cs]), (4, 8, 128, wcp[:, :, cs]), (8, 12, 64, wdp[:, :, cs])]
                for b in range(3):
                    c0 = 5376 + b * 1024 + oc * 128
                    srcs.append((12 + 8 * b, 20 + 8 * b, 128, win[:, :, c0:c0 + 128]))
                for si, (a0, a1, npart, src) in enumerate(srcs):
                    P.op("sp", lambda e, st=st, a0=a0, a1=a1, npart=npart, src=src: e.dma_start(out=st[0:npart, a0:a1, :], in_=src),
                         w=[sttok + "s%d" % si], dma=sttok + "s%d" % si)
                if oc < 2:
                    P.op("pool", lambda e, st=st: e.memset(st[64:128, 8:12, :], 0.0), w=[sttok + "z"])
                if oc % 2 == 0:
                    P.op("act", lambda e, st=st, bf=bf: e.copy(bf[:], st[:]),
                         r=[sttok + "s%d" % si for si in range(6)] + [sttok + "z"], w=[bftok])
                else:
                    P.op("dve", lambda e, st=st, bf=bf: e.tensor_copy(bf[:], st[:]),
                         r=[sttok + "s%d" % si for si in range(6)] + [sttok + "z"], w=[bftok])
                for tt in range(4):
                    ts = slice(tt * 512, (tt + 1) * 512)
                    gb = []
                    for b in range(3):
                        bank, btok = ps.next()
                        mm_group(P, bank[:, :], [(bf[:, 12 + 8 * b + kc, :], xT[:, kc, ts]) for kc in range(8)], r=[bftok], w=[btok])
                        s_, stok_ = sg.next()
                        P.op("act", lambda e, s_=s_, bank=bank: e.activation(out=s_[:], in_=bank[:, :], func=AF.Sigmoid), r=[btok], w=[stok_])
                        gb.append((s_, stok_))
                    pb = []
                    for (a0, n, npart, src, srct) in [(0, 4, 128, ymT, ytoks), (4, 4, 128, ycT, []), (8, 4, 64, ydT, ytoks)]:
                        bank, btok = ps.next()
                        mm_group(P, bank[:, :], [(bf[0:npart, a0 + kc, :], src[0:npart, kc, ts]) for kc in range(n)], r=[bftok] + srct, w=[btok])
                        pb.append((bank, btok))
                    ms = []
                    for b in range(3):
                        m_, mtok_ = mt.next()
                        P.op("dve", lambda e, m_=m_, b=b, gb=gb, pb=pb: e.tensor_tensor(m_[:], gb[b][0][:], pb[b][0][:, :], ALU.mult),
                             r=[gb[b][1], pb[b][1]], w=[mtok_])
                        ms.append((m_, mtok_))
                    P.op("pool", lambda e, ms=ms: e.tensor_tensor(ms[0][0][:], ms[0][0][:], ms[1][0][:], ALU.add),
                         r=[ms[0][1], ms[1][1]], w=[ms[0][1]])
                    P.op("pool", lambda e, ms=ms, oc=oc, ts=ts: e.tensor_tensor(mergedT[:, oc, ts], ms[0][0][:], ms[2][0][:], ALU.add),
                         r=[ms[0][1], ms[2][1]], w=["mT%d_%d" % (oc, tt)])
            P.emit()

        if "dbg_m" in D:
            for kc in range(8):
                P.op("sp", lambda e, kc=kc: e.dma_start(out=D["dbg_m"][kc * 128:(kc + 1) * 128, :], in_=mergedT[:, kc, :]), w=["dbgm%d" % kc], dma="dbgm")
            for kc in range(4):
                P.op("sp", lambda e, kc=kc: e.dma_start(out=D["dbg_yc"][kc * 128:(kc + 1) * 128, :], in_=ycT[:, kc, :]), w=["dbgy%d" % kc], dma="dbgy")
            P.emit()
        if STOP <= 2:
            return
        with contextlib.ExitStack() as es:
            x1T = es.enter_context(nc.sbuf_tensor("p3_x1T", [128, 8, TOK], BF16))
            wmo = es.enter_context(nc.sbuf_tensor("p3_wmo", [128, 8, 1024], BF16))
            ws = WStream(nc, es, "p3_wmo_s", [128, 8, 512], nstage=1, nbf=1)
            ln = LNCtx(nc, es, "ln1", D["ln1_g"][layer], D["ln1_b"][layer], ident)
            ln.load_consts(P)
            wmov = D["w_mix_out"][layer].rearrange("(kc p) n -> p kc n", p=128)
            for hf in range(2):
                wb, wbtok = ws.load(P, wmov[:, :, hf * 512:(hf + 1) * 512])
                P.op("pool", lambda e, wb=wb, hf=hf: e.tensor_copy(wmo[:, :, hf * 512:(hf + 1) * 512], wb[:]), r=[wbtok], w=["wmo%d" % hf])
            def mix_mm(t16):
                tsl = slice(t16 * 128, (t16 + 1) * 128)
                banks, btoks = [], []
                for hf in range(2):
                    bank, btok = ps.next()
                    mm_group(P, bank[:, :], [(mergedT[:, kc, tsl], wmo[:, kc, hf * 512:(hf + 1) * 512]) for kc in range(8)],
                             r=["wmo%d" % hf], w=[btok])
                    banks.append(bank)
                    btoks.append(btok)
                return banks, btoks

            nxt = mix_mm(0)
            for t16 in range(16):
                tsl = slice(t16 * 128, (t16 + 1) * 128)
                banks, btoks = nxt
                if t16 + 1 < 16:
                    nxt = mix_mm(t16 + 1)
                ln.tile(P, banks, btoks, D[x_in_name][tsl, :], D["xres_b"][tsl, :], "xres_b%d" % t16,
                        x1T[:, :, tsl], "x1T%d" % t16, ps)
            alltoks = ["x1T%d%s" % (t, s) for t in range(16) for s in "ab"]
            for kc in range(8):
                P.op("sp" if kc % 2 else "pool", lambda e, kc=kc: e.dma_start(out=D["xT_b"][kc * 128:(kc + 1) * 128, :], in_=x1T[:, kc, :]),
                     r=alltoks, w=["xT_b%d" % kc], dma="st_x1T%d" % kc)
            xz = es.enter_context(nc.sbuf_tensor("p3_xz", [128, 16], BF16))
            P.op("pool", lambda e: e.memset(xz[:], 0.0), w=["xz"])
            P.op("sp", lambda e: e.dma_start(out=D["x3in"][0:128, :].rearrange("p (k t) -> p k t", k=8), in_=x1T[:, :, TOK - 2:TOK]),
                 r=["x1T15a", "x1T15b"], w=["x3in"], dma="st_x3")
            P.op("sp", lambda e: e.dma_start(out=D["x3in"][128:256, :], in_=xz[:]), r=["xz"], w=["x3inz"], dma="st_x3")
            P.emit()


def host_small_layouts(inputs):
    wsc = np.ascontiguousarray(np.asarray(inputs["w_short_conv"], np.float32).reshape(DEPTH, 3, 4, 128).transpose(0, 3, 2, 1))
    wfc = np.ascontiguousarray(np.asarray(inputs["w_ffn_conv"], np.float32).reshape(DEPTH, 3, 44, 128).transpose(0, 3, 2, 1))
    bfc = np.ascontiguousarray(np.asarray(inputs["b_ffn_conv"], np.float32).reshape(DEPTH, 44, 128).transpose(0, 2, 1))
    return wsc, wfc, bfc


def phase_p3b(nc, P, D, layer, out_name, write_xT):
    wup = D["w_up"][layer].rearrange("(kc p) n -> p kc n", p=128)
    wdn = D["w_down"][layer].rearrange("(i p) n -> p i n", p=128)
    xTb = D["xT_b"].rearrange("(kc p) t -> p kc t", p=128)
    NI = D_FF // 128
    with contextlib.ExitStack() as es0:
        hT = es0.enter_context(nc.sbuf_tensor("f_hT", [128, NI, TOK], BF16))
        sidx = es0.enter_context(nc.sbuf_tensor("sb_idx4", [128, NIDX], I32))
        ident = es0.enter_context(nc.sbuf_tensor("sb_ident4", [128, 128], BF16))
        ps = Ring(nc, es0, "f_ps", [128, 512], F32, 7, psum=True)
        psh = Ring(nc, es0, "f_psh", [128, 512], F32, 1, psum=True)
        with contextlib.ExitStack() as es:
            xT = es.enter_context(nc.sbuf_tensor("f_xT", [128, 8, 16 + TOK], BF16))
            wfc = es.enter_context(nc.sbuf_tensor("f_wfc", [128, 44, 3], F32))
            bfc = es.enter_context(nc.sbuf_tensor("f_bfc", [128, 44], F32))
            xh = es.enter_context(nc.sbuf_tensor("f_xh", [128, 16], BF16))
            wst = Ring(nc, es, "f_wst", [128, 8, 256], F32, 2)
            wbf = Ring(nc, es, "f_wbf", [128, 8, 256], BF16, 3)
            ug = Ring(nc, es, "f_ug", [128, 16 + TOK], F32, 2)
            uv = Ring(nc, es, "f_uv", [128, 16 + TOK], F32, 2)
            ag = Ring(nc, es, "f_ag", [128, TOK], F32, 1)
            av = Ring(nc, es, "f_av", [128, TOK], F32, 1)
            P.op("sp", lambda e: e.dma_start(out=sidx[:], in_=D["idx"][:, :]), w=["idx"], dma="c_idx")
            P.op("sp", lambda e: e.dma_start(out=ident[:], in_=D["ident"][:, :]), w=["ident"], dma="c_ident")
            P.op("sp", lambda e: e.dma_start(out=wfc[:], in_=D["wfc"][layer]), w=["wfc"], dma="c_wfc")
            P.op("sp", lambda e: e.dma_start(out=bfc[:], in_=D["bfc"][layer]), w=["bfc"], dma="c_bfc")
            P.op("pool", lambda e: e.indirect_dma_start(
                out=xh[:], out_offset=None, in_=D["x3g"][:, :],
                in_offset=bass.IndirectOffsetOnAxis(ap=sidx[0:128, IDX_HALO:IDX_HALO + 1], axis=0)),
                r=["idx", "x3g"], w=["xh"], dma="ld_xh")
            for kc in range(8):
                P.op("sp", lambda e, kc=kc: e.dma_start(out=xT[:, kc, 16:16 + TOK], in_=D["xT_b"][kc * 128:(kc + 1) * 128, :]),
                     w=["xT%d" % kc], dma="ld_xT%d" % kc)
            P.op("dve", lambda e: e.tensor_copy(xT[:, :, 14:16], xh[:].rearrange("p (k t) -> p k t", k=8)), r=["xh"], w=["xTh"])
            xr = ["xT%d" % kc for kc in range(8)]
            wq = {}

            def prefetch_w(i):
                st, sttok = wst.next()
                wb, wbtok = wbf.next()
                P.op("sp", lambda e, st=st, i=i: e.dma_start(out=st[:, :, 0:128], in_=wup[:, :, i * 128:(i + 1) * 128]), w=[sttok + "g"], dma=sttok + "g")
                P.op("sp", lambda e, st=st, i=i: e.dma_start(out=st[:, :, 128:256], in_=wup[:, :, D_FF + i * 128:D_FF + (i + 1) * 128]), w=[sttok + "v"], dma=sttok + "v")
                if i % 2 == 0:
                    P.op("act", lambda e, st=st, wb=wb: e.copy(wb[:], st[:]), r=[sttok + "g", sttok + "v"], w=[wbtok])
                else:
                    P.op("dve", lambda e, st=st, wb=wb: e.tensor_copy(wb[:], st[:]), r=[sttok + "g", sttok + "v"], w=[wbtok])
                wq[i] = (wb, wbtok)

            prefetch_w(0)
            prefetch_w(1)
            for i in range(NI):
                if i + 2 < NI:
                    prefetch_w(i + 2)
                wb, wbtok = wq.pop(i)
                u_g, ugtok = ug.next()
                u_v, uvtok = uv.next()
                H, htok = psh.next()
                mm_group(P, H[:, 0:2], [(wb[:, kc, 0:128], xT[:, kc, 14:16]) for kc in range(8)], r=[wbtok, "xTh"], w=[htok + "g"])
                mm_group(P, H[:, 2:4], [(wb[:, kc, 128:256], xT[:, kc, 14:16]) for kc in range(8)], r=[wbtok, "xTh"], w=[htok + "v"])
                P.op("dve", lambda e, u_g=u_g, H=H: e.tensor_copy(u_g[:, 14:16], H[:, 0:2]), r=[htok + "g"], w=[ugtok + "h"])
                P.op("dve", lambda e, u_v=u_v, H=H: e.tensor_copy(u_v[:, 14:16], H[:, 2:4]), r=[htok + "v"], w=[uvtok + "h"])
                for tg in range(4):
                    cs = slice(16 + tg * 512, 16 + (tg + 1) * 512)
                    G, gtok = ps.next()
                    V, vtok = ps.next()
                    mm_group(P, G[:, :], [(wb[:, kc, 0:128], xT[:, kc, cs]) for kc in range(8)], r=[wbtok] + xr, w=[gtok])
                    mm_group(P, V[:, :], [(wb[:, kc, 128:256], xT[:, kc, cs]) for kc in range(8)], r=[wbtok] + xr, w=[vtok])
                    P.op("act", lambda e, u_g=u_g, G=G, cs=cs: e.copy(u_g[:, cs], G[:, :]), r=[gtok], w=[ugtok + "d%d" % tg])
                    P.op("act", lambda e, u_v=u_v, V=V, cs=cs: e.copy(u_v[:, cs], V[:, :]), r=[vtok], w=[uvtok + "d%d" % tg])
                outs = []
                for (u_, utok, aring, ci) in [(u_g, ugtok, ag, i), (u_v, uvtok, av, NI + i)]:
                    a_, atok = aring.next()
                    ur = [utok + "d%d" % t for t in range(4)]
                    P.op("act", lambda e, u_=u_, a_=a_, ci=ci: e.activation(out=a_[:], in_=u_[:, 16:16 + TOK], func=AF.Identity,
                                                                             bias=bfc[:, ci:ci + 1], scale=wfc[:, ci, 0:1]),
                         r=ur + ["wfc", "bfc"], w=[atok + "0"])
                    P.op("dve", lambda e, u_=u_, a_=a_, ci=ci: e.scalar_tensor_tensor(out=a_[:], in0=u_[:, 15:15 + TOK], scalar=wfc[:, ci, 1:2],
                                                                                     in1=a_[:], op0=ALU.mult, op1=ALU.add),
                         r=ur + [utok + "h", atok + "0"], w=[atok + "1"])
                    P.op("dve", lambda e, u_=u_, a_=a_, ci=ci: e.scalar_tensor_tensor(out=a_[:], in0=u_[:, 14:14 + TOK], scalar=wfc[:, ci, 2:3],
                                                                                     in1=a_[:], op0=ALU.mult, op1=ALU.add),
                         r=ur + [utok + "h", atok + "1"], w=[atok])
                    outs.append((a_, atok))
                P.op("act", lambda e, a_=outs[0][0]: e.activation(out=a_[:], in_=a_[:], func=AF.Silu), r=[outs[0][1]], w=[outs[0][1] + "s"])
                P.op("pool", lambda e, s_=outs[0][0], a_=outs[1][0], i=i: e.tensor_tensor(hT[:, i, :], s_[:], a_[:], ALU.mult),
                     r=[outs[0][1] + "s", outs[1][1]], w=["hT%d" % i])
            P.emit()
        with contextlib.ExitStack() as es:
            wdnb = es.enter_context(nc.sbuf_tensor("f_wdnb", [128, NI, 1024], BF16))
            dst_ = Ring(nc, es, "f_dst", [128, 1024], F32, 2)
            x2T = Ring(nc, es, "f_x2T", [128, 8, 512], BF16, 2)
            ln = LNCtx(nc, es, "ln2", D["ln2_g"][layer], D["ln2_b"][layer], ident)
            ln.load_consts(P)
            for i in range(NI):
                st, sttok = dst_.next()
                P.op("sp", lambda e, st=st, i=i: e.dma_start(out=st[:], in_=wdn[:, i, :]), w=[sttok], dma=sttok)
                if i % 2 == 0:
                    P.op("act", lambda e, st=st, i=i: e.copy(wdnb[:, i, :], st[:]), r=[sttok], w=["wdnb%d" % i])
                else:
                    P.op("dve", lambda e, st=st, i=i: e.tensor_copy(wdnb[:, i, :], st[:]), r=[sttok], w=["wdnb%d" % i])
            def down_mm(t16):
                tsl = slice(t16 * 128, (t16 + 1) * 128)
                banks, btoks = [], []
                for hf in range(2):
                    bank, btok = ps.next()
                    mm_group(P, bank[:, :], [(hT[:, i, tsl], wdnb[:, i, hf * 512:(hf + 1) * 512]) for i in range(NI)],
                             r=[], w=[btok], rk=[["wdnb%d" % i] for i in range(NI)])
                    banks.append(bank)
                    btoks.append(btok)
                return banks, btoks

            nxt = down_mm(0)
            for tg in range(4):
                xo, xotok = x2T.next()
                for tq in range(4):
                    t16 = tg * 4 + tq
                    tsl = slice(t16 * 128, (t16 + 1) * 128)
                    banks, btoks = nxt
                    if t16 + 1 < 16:
                        nxt = down_mm(t16 + 1)
                    ln.tile(P, banks, btoks, D["xres_b"][tsl, :], D[out_name][tsl, :], "yout%d" % t16,
                            xo[:, :, tq * 128:(tq + 1) * 128], xotok + "_%d" % tq, ps)
                if write_xT:
                    P.op("sp", lambda e, xo=xo, tg=tg: e.dma_start(out=D["xT_a"].rearrange("(kc p) t -> p kc t", p=128)[:, :, tg * 512:(tg + 1) * 512], in_=xo[:]),
                         r=[xotok + "_%d%s" % (tq, s_) for tq in range(4) for s_ in "ab"], w=["xT_a_%d" % tg], dma=xotok)
            P.emit()


GROUPS = [[0, 1, 2, 3], [4, 5, 6, 7]]
W_P3A = ["w_moba_proj", "w_dil_proj", "w_conv_proj", "w_mix_out", "ln1_g", "ln1_b"]
W_P3B = ["w_up", "w_down", "ln2_g", "ln2_b"]
_PROG_CACHE = {}


def _prog(key, builder):
    if key not in _PROG_CACHE:
        _PROG_CACHE[key] = builder()
    return _PROG_CACHE[key]


def _build_p1(first):
    nc = bass.Bass("TRN2", target_bir_lowering=False)
    D = make_D(nc, ["w_in", "xT_f32" if first else "xT_a"], (["xT_a"] if first else []) + ["x1in", "u_scr", "uh_in"], layers=1)
    phase_p1(nc, Prog(nc), D, 0, first)
    return nc


def _build_p2():
    nc = bass.Bass("TRN2", target_bir_lowering=False)
    D = make_D(nc, ["x1g", "idx", "blkind", "gtab", "dmask", "ident", "dilmask"], ["x2in"])
    P = Prog(nc)
    phase_p2a(nc, P, D)
    phase_p2b(nc, P, D)
    return nc


def _build_p3a():
    nc = bass.Bass("TRN2", target_bir_lowering=False)
    D = make_D(nc, ["w_in", "xT_a", "x_tm", "x2g", "u_scr", "uh_g", "idx", "ident", "wsc"] + W_P3A, ["xres_b", "xT_b", "x3in"], layers=1)
    phase_p3a(nc, Prog(nc), D, 0, "x_tm")
    return nc


def _build_p3b():
    nc = bass.Bass("TRN2", target_bir_lowering=False)
    D = make_D(nc, ["xT_b", "xres_b", "x3g", "idx", "ident", "wfc", "bfc"] + W_P3B, ["xres_a", "xT_a"], ["wup_bf"], layers=1)
    phase_p3b(nc, Prog(nc), D, 0, "xres_a", True)
    return nc


def _run(nc, in_maps):
    res = run_bass_kernel_spmd(nc, in_maps, core_ids=list(range(8)))
    return res.results


def _gather_groups(per_core, ch=None):
    out = []
    n = per_core[0].shape[0]
    ch = ch or n
    for c in range(8):
        g = c // 4
        out.append(np.concatenate([per_core[g * 4 + r][k:k + ch] for k in range(0, n, ch) for r in range(4)], 0))
    return out


def kernel_unfused(inputs):
    x = np.asarray(inputs["x"], np.float32)
    wsc, wfc, bfc = host_small_layouts(inputs)
    C = host_consts()
    idx = [host_idx(c) for c in range(8)]
    W = {k: np.asarray(v, np.float32) for k, v in inputs.items()}
    xs = [np.ascontiguousarray(x[c // 4, (c % 4) * TOK:(c % 4 + 1) * TOK]) for c in range(8)]
    x_tm = xs
    xT_a = None
    for l in range(DEPTH):
        sl = slice(l, l + 1)
        if l == 0:
            r = _run(_prog("p1f", lambda: _build_p1(True)),
                     [{"w_in": W["w_in"][sl], "xT_f32": np.ascontiguousarray(xs[c].T)} for c in range(8)])
            xT_a = [r[c]["xT_a"] for c in range(8)]
        else:
            r = _run(_prog("p1n", lambda: _build_p1(False)), [{"w_in": W["w_in"][sl], "xT_a": xT_a[c]} for c in range(8)])
        x1g = _gather_groups([r[c]["x1in"] for c in range(8)], CH1)
        uh_g = _gather_groups([r[c]["uh_in"] for c in range(8)])
        u_scr = [r[c]["u_scr"] for c in range(8)]
        r = _run(_prog("p2", _build_p2),
                 [{"x1g": x1g[c], "idx": idx[c], "blkind": C["blkind"], "gtab": C["gtab"], "dmask": C["dmask"],
                   "ident": C["ident"], "dilmask": C["dilmask"]} for c in range(8)])
        x2g = _gather_groups([r[c]["x2in"] for c in range(8)], CH2)
        maps = []
        for c in range(8):
            m = {"w_in": W["w_in"][sl], "xT_a": xT_a[c], "x_tm": x_tm[c], "x2g": x2g[c], "u_scr": u_scr[c], "uh_g": uh_g[c],
                 "idx": idx[c], "ident": C["ident"], "wsc": wsc[sl]}
            for nm in W_P3A:
                m[nm] = W[nm][sl]
            maps.append(m)
        r = _run(_prog("p3a", _build_p3a), maps)
        x3g = _gather_groups([r[c]["x3in"] for c in range(8)])
        maps = []
        for c in range(8):
            m = {"xT_b": r[c]["xT_b"], "xres_b": r[c]["xres_b"], "x3g": x3g[c], "idx": idx[c], "ident": C["ident"],
                 "wfc": wfc[sl], "bfc": bfc[sl]}
            for nm in W_P3B:
                m[nm] = W[nm][sl]
            maps.append(m)
        r = _run(_prog("p3b", _build_p3b), maps)
        x_tm = [r[c]["xres_a"] for c in range(8)]
        xT_a = [r[c]["xT_a"] for c in range(8)]
    out = np.zeros((2, SEQ, D_MODEL), np.float32)
    for c in range(8):
        out[c // 4, (c % 4) * TOK:(c % 4 + 1) * TOK] = x_tm[c]
    return out


class NCX:
    _uid = [0]

    def __init__(self, nc):
        object.__setattr__(self, "_nc", nc)
        NCX._uid[0] += 1
        object.__setattr__(self, "_sfx", "_u%d" % NCX._uid[0])

    def __getattr__(self, k):
        return getattr(self._nc, k)

    def sbuf_tensor(self, name, *a, **kw):
        return self._nc.sbuf_tensor(name + self._sfx, *a, **kw)

    def psum_tensor(self, name, *a, **kw):
        return self._nc.psum_tensor(name + self._sfx, *a, **kw)


FUSED_IN = ["x_tm", "xT_f32", "w_in", "idx", "blkind", "gtab", "dmask", "ident", "dilmask", "wsc", "wfc", "bfc"] + W_P3A + W_P3B
FUSED_INT = ["xT_a", "xT_b", "x1in", "x1g", "u_scr", "uh_in", "uh_g", "x2in", "x2g", "xres_a", "xres_b", "x3in", "x3g"]


def _build_fused():
    nc = bass.Bass("TRN2", target_bir_lowering=False)
    D = make_D(nc, FUSED_IN, ["out"], FUSED_INT, layers=DEPTH)
    P = Prog(nc)

    import os
    NOCC = os.environ.get("NOCC", "")

    def allgather(src, dst, tag, ch=None, order=None):
        if NOCC == "all" or tag in NOCC.split(","):
            return
        if ch is not None:
            n = D[src].shape[0]
            for k in (order or range(n // ch)):
                P.op("pool", lambda e, k=k: e.collective_compute(
                    "AllGather", ALU.bypass, replica_groups=GROUPS,
                    ins=[D[src][k * ch:(k + 1) * ch, :]], outs=[D[dst][k * 4 * ch:(k + 1) * 4 * ch, :]]),
                    w=["%s_c%d" % (dst, k)], cc="%s_%d" % (tag, k))
            return
        P.op("pool", lambda e: e.collective_compute("AllGather", ALU.bypass, replica_groups=GROUPS,
                                                    ins=[D[src]], outs=[D[dst]]), w=[dst], cc=tag)

    for l in range(DEPTH):
        last = l == DEPTH - 1
        phase_p1(NCX(nc), P, D, l, l == 0)
        allgather("x1in", "x1g", "ag_x1", CH1, order=[0, 1, 2, 3, 10, 11, 4, 5, 6, 7, 8, 9, 12, 13, 14, 15])
        allgather("uh_in", "uh_g", "ag_uh")
        phase_p2a(NCX(nc), P, D)
        phase_p2b(NCX(nc), P, D)
        allgather("x2in", "x2g", "ag_x2", CH2)
        phase_p3a(NCX(nc), P, D, l, "x_tm" if l == 0 else "xres_a")
        allgather("x3in", "x3g", "ag_x3")
        phase_p3b(NCX(nc), P, D, l, "out" if last else "xres_a", not last)
    return nc


def kernel_fused(inputs):
    x = np.asarray(inputs["x"], np.float32)
    wsc, wfc, bfc = host_small_layouts(inputs)
    C = host_consts()
    W = {k: np.asarray(v, np.float32) for k, v in inputs.items()}
    maps = []
    for c in range(8):
        xc = np.ascontiguousarray(x[c // 4, (c % 4) * TOK:(c % 4 + 1) * TOK])
        m = {"x_tm": xc, "xT_f32": np.ascontiguousarray(xc.T), "w_in": W["w_in"], "idx": host_idx(c),
             "blkind": C["blkind"], "gtab": C["gtab"], "dmask": C["dmask"], "ident": C["ident"], "dilmask": C["dilmask"],
             "wsc": wsc, "wfc": wfc, "bfc": bfc}
        for nm in W_P3A + W_P3B:
            m[nm] = W[nm]
        maps.append(m)
    r = _run(_prog("fused", _build_fused), maps)
    out = np.zeros((2, SEQ, D_MODEL), np.float32)
    for c in range(8):
        out[c // 4, (c % 4) * TOK:(c % 4 + 1) * TOK] = r[c]["out"]
    return out


def kernel(**inputs):
    return kernel_fused(inputs)
```
